# Optimizing a Trainium2 kernel written in Bass

```python
import jax, jax.numpy as jnp
from jax import lax
import numpy as np

D_MODEL = 2048
BATCH = 4
SEQ = 2048
DEPTH = 2
DEC_BATCH = 128
DEC_SEQ = 8
PAST_LEN = 16384
PAGE_SIZE = 128

D_MIX = D_MODEL
POOL_WIDTH = D_MIX // 2
POOL_WINDOWS = (2, 4, 8, 16)
POOL_GROUPS = len(POOL_WINDOWS)
POOL_GROUP_DIM = POOL_WIDTH // POOL_GROUPS
POOL_BUF = max(POOL_WINDOWS) - 1
RET_WIDTH = D_MIX - POOL_WIDTH
RET_HEADS = 4
RET_DK = RET_WIDTH // RET_HEADS
RET_DV = RET_WIDTH // RET_HEADS
RET_CHUNK = 128
ROPE_BASE = 10000.0
PLE_DIM = 256
EPS = 1e-6
SPLITS = (POOL_WIDTH, 2 * POOL_WIDTH, 2 * POOL_WIDTH + RET_HEADS * RET_DK,
          2 * POOL_WIDTH + 2 * RET_HEADS * RET_DK, 2 * POOL_WIDTH + 2 * RET_HEADS * RET_DK + RET_WIDTH)
IN_COLS = 2 * POOL_WIDTH + 2 * RET_HEADS * RET_DK + 2 * RET_WIDTH

kernel_name = "hybrid_pool_retention_decoder_step"


def rmsnorm(x, g):
    x32 = x.astype(jnp.float32)
    y = x32 * lax.rsqrt(jnp.mean(x32 * x32, axis=-1, keepdims=True) + EPS) * g.astype(jnp.float32)
    return y.astype(x.dtype)


def head_rmsnorm(x):
    return x * lax.rsqrt(jnp.mean(x * x, axis=-1, keepdims=True) + EPS)


def rope(x, pos):
    half = x.shape[-1] // 2
    inv = ROPE_BASE ** (-jnp.arange(half, dtype=jnp.float32) / half)
    ang = pos[:, None] * inv[None, :]
    cos = jnp.cos(ang)[None, :, None, :]
    sin = jnp.sin(ang)[None, :, None, :]
    x1, x2 = x[..., :half], x[..., half:]
    return jnp.concatenate([x1 * cos - x2 * sin, x2 * cos + x1 * sin], axis=-1)


def retention_log_decay():
    return jnp.log(1.0 - 2.0 ** (-5.0 - jnp.arange(RET_HEADS, dtype=jnp.float32)))


def retention_chunk(qc, kc, vc, S, log_g):
    C = qc.shape[1]
    idx = jnp.arange(C, dtype=jnp.float32)
    diff = idx[:, None] - idx[None, :]
    dmask = jnp.where(diff[None] >= 0,
                      jnp.exp(jnp.maximum(diff, 0.0)[None] * log_g[:, None, None]), 0.0)
    scores = jnp.einsum('bqhd,bkhd->bhqk', qc, kc) * dmask[None]
    intra = jnp.einsum('bhqk,bkhe->bqhe', scores, vc)
    q_decay = jnp.exp((idx + 1.0)[:, None] * log_g[None, :])
    inter = jnp.einsum('bqhd,bhde->bqhe', qc, S) * q_decay[None, :, :, None]
    k_decay = jnp.exp((C - 1.0 - idx)[:, None] * log_g[None, :])
    S_new = jnp.exp(C * log_g)[None, :, None, None] * S + jnp.einsum(
        'bkhd,bkhe->bhde', kc * k_decay[None, :, :, None], vc)
    return intra + inter, S_new


def retention(q, k, v, S0, start):
    B, L = q.shape[0], q.shape[1]
    C = RET_CHUNK if L % RET_CHUNK == 0 else L
    NC = L // C
    log_g = retention_log_decay()
    pos = (jnp.arange(L) + start).astype(jnp.float32)
    q = rope(q.astype(jnp.float32), pos)
    k = rope(k.astype(jnp.float32), pos) * (RET_DK ** -0.5)
    v = v.astype(jnp.float32)

    def to_chunks(a):
        return a.reshape(B, NC, C, RET_HEADS, a.shape[-1]).transpose(1, 0, 2, 3, 4)

    def step(S, xs):
        qc, kc, vc = xs
        o, S_new = retention_chunk(qc, kc, vc, S, log_g)
        return S_new, o

    S_fin, ys = lax.scan(step, S0.astype(jnp.float32), (to_chunks(q), to_chunks(k), to_chunks(v)))
    out = ys.transpose(1, 0, 2, 3, 4).reshape(B, L, RET_HEADS, RET_DV)
    return out, S_fin


def pool_mixer(u, buf, start, w_pool, pool_scale):
    B, L, _ = u.shape
    u32 = u.astype(jnp.float32)
    ext = jnp.concatenate([buf.astype(jnp.float32), u32], axis=1)
    cs = jnp.concatenate([jnp.zeros((B, 1, POOL_WIDTH), jnp.float32),
                          jnp.cumsum(ext, axis=1)], axis=1)
    t_abs = jnp.arange(L) + start
    end = POOL_BUF + 1
    means = []
    for g, w in enumerate(POOL_WINDOWS):
        sl = slice(g * POOL_GROUP_DIM, (g + 1) * POOL_GROUP_DIM)
        s = cs[:, end:end + L, sl] - cs[:, end - w:end - w + L, sl]
        cnt = jnp.minimum(t_abs + 1, w).astype(jnp.float32)[None, :, None]
        means.append(s / cnt)
    pooled = jnp.concatenate(means, axis=-1) - u32
    pooled = pooled.reshape(B, L, POOL_GROUPS, POOL_GROUP_DIM)
    mixed = jnp.einsum('blgc,gcd->blgd', pooled, w_pool.astype(jnp.float32)).reshape(B, L, POOL_WIDTH)
    y = mixed * pool_scale.astype(jnp.float32)
    new_buf = ext[:, -POOL_BUF:].astype(buf.dtype)
    return y, new_buf


def trunk_layer(x, p, S0, buf0, start, w_in, w_pool, pool_scale, w_out,
                pre_g, post_g, w_ple, ple_g, w_ple_gate, b_ple_gate):
    B, L, _ = x.shape
    h = rmsnorm(x, pre_g)
    proj = h @ w_in
    u, gp, q, k, v, gr = jnp.split(proj, SPLITS, axis=-1)
    pool_y, buf_new = pool_mixer(u, buf0, start, w_pool, pool_scale)
    pool_y = pool_y * jax.nn.silu(gp.astype(jnp.float32))
    ret_y, S_new = retention(q.reshape(B, L, RET_HEADS, RET_DK), k.reshape(B, L, RET_HEADS, RET_DK),
                             v.reshape(B, L, RET_HEADS, RET_DV), S0, start)
    ret_y = head_rmsnorm(ret_y).reshape(B, L, RET_WIDTH) * jax.nn.silu(gr.astype(jnp.float32))
    mix = jnp.concatenate([pool_y, ret_y], axis=-1).astype(x.dtype) @ w_out
    x = x + rmsnorm(mix, post_g)
    ple = rmsnorm(p @ w_ple, ple_g)
    gate = jax.nn.sigmoid((x @ w_ple_gate + b_ple_gate).astype(jnp.float32))
    x = x + (gate * ple.astype(jnp.float32)).astype(x.dtype)
    return x, S_new, buf_new


def run_trunk(x, p, S_all, buf_all, start, w_in, w_pool, pool_scale, w_out,
              pre_g, post_g, w_ple, ple_g, w_ple_gate, b_ple_gate):
    S_out, buf_out = [], []
    for i in range(DEPTH):
        x, S_i, b_i = trunk_layer(x, p[i], S_all[i], buf_all[i], start, w_in[i], w_pool[i],
                                  pool_scale[i], w_out[i], pre_g[i], post_g[i], w_ple[i],
                                  ple_g[i], w_ple_gate[i], b_ple_gate[i])
        S_out.append(S_i)
        buf_out.append(b_i)
    return x, jnp.stack(S_out, axis=0), jnp.stack(buf_out, axis=0)


def setup_inputs(seed: int = 0) -> dict:
    key = jax.random.key(seed)
    ks = jax.random.split(key, 20)
    f32 = jnp.float32
    nrm = lambda k, shape, s: jax.random.normal(k, shape, f32) * s
    return {
        "x_prompt": nrm(ks[0], (BATCH, SEQ, D_MODEL), 1.0),
        "x_sample": nrm(ks[1], (DEC_BATCH, DEC_SEQ, D_MODEL), 1.0),
        "state_ret": nrm(ks[2], (DEPTH, DEC_BATCH, RET_HEADS, RET_DK, RET_DV), 0.5),
        "state_pool": nrm(ks[3], (DEPTH, DEC_BATCH, POOL_BUF, POOL_WIDTH), 1.0),
        "p_prompt": nrm(ks[4], (DEPTH, BATCH, SEQ, PLE_DIM), 1.0),
        "p_sample": nrm(ks[5], (DEPTH, DEC_BATCH, DEC_SEQ, PLE_DIM), 1.0),
        "w_in": nrm(ks[6], (DEPTH, D_MODEL, IN_COLS), D_MODEL ** -0.5),
        "w_pool": nrm(ks[7], (DEPTH, POOL_GROUPS, POOL_GROUP_DIM, POOL_GROUP_DIM), POOL_GROUP_DIM ** -0.5),
        "pool_scale": 1.0 + nrm(ks[8], (DEPTH, POOL_WIDTH), 0.02),
        "w_out": nrm(ks[9], (DEPTH, D_MIX, D_MODEL), D_MIX ** -0.5),
        "pre_g": 1.0 + nrm(ks[10], (DEPTH, D_MODEL), 0.02),
        "post_g": 1.0 + nrm(ks[11], (DEPTH, D_MODEL), 0.02),
        "w_ple": nrm(ks[12], (DEPTH, PLE_DIM, D_MODEL), PLE_DIM ** -0.5),
        "ple_g": 1.0 + nrm(ks[13], (DEPTH, D_MODEL), 0.02),
        "w_ple_gate": nrm(ks[14], (DEPTH, D_MODEL, D_MODEL), D_MODEL ** -0.5),
        "b_ple_gate": nrm(ks[15], (DEPTH, D_MODEL), 0.02),
    }


def reference(x_prompt, x_sample, state_ret, state_pool, p_prompt, p_sample,
              w_in, w_pool, pool_scale, w_out, pre_g, post_g, w_ple, ple_g, w_ple_gate, b_ple_gate):
    S0_p = jnp.zeros((DEPTH, BATCH, RET_HEADS, RET_DK, RET_DV), jnp.float32)
    buf0_p = jnp.zeros((DEPTH, BATCH, POOL_BUF, POOL_WIDTH), x_prompt.dtype)
    y_prompt, new_ret_prompt, new_pool_prompt = run_trunk(
        x_prompt, p_prompt, S0_p, buf0_p, 0, w_in, w_pool, pool_scale, w_out,
        pre_g, post_g, w_ple, ple_g, w_ple_gate, b_ple_gate)
    y_sample, new_ret_sample, new_pool_sample = run_trunk(
        x_sample, p_sample, state_ret, state_pool, PAST_LEN, w_in, w_pool, pool_scale, w_out,
        pre_g, post_g, w_ple, ple_g, w_ple_gate, b_ple_gate)
    return (y_prompt, y_sample, new_ret_prompt, new_pool_prompt, new_ret_sample, new_pool_sample)
```

```python
import numpy as np
import ml_dtypes
import concourse.bass as bass
import concourse.mybir as mybir
from concourse.bass_utils import run_bass_kernel_spmd

F32 = mybir.dt.float32
BF16 = mybir.dt.bfloat16
ALU = mybir.AluOpType
AF = mybir.ActivationFunctionType
AX = mybir.AxisListType

D = 2048
NT = 9
TOK = NT * 128
INC = 6144
DEPTH = 2
EPS = 1e-6
GAM = [1.0 - 2.0 ** (-5.0 - h) for h in range(4)]
WINS = (2, 4, 8, 16)
NCORE = 8
DBG = {}
SAME_ENGINE_WAITS = DBG.get("sew", True)


class Op:
    __slots__ = ("eng", "fn", "deps", "kind", "sig")

    def __init__(self, eng, fn, deps, kind):
        self.eng, self.fn, self.deps, self.kind, self.sig = eng, fn, deps, kind, None


class Sched:
    KRING = {"sp": 12, "pool": 6}

    def __init__(self):
        self.ops = []
        self.lastw = {}
        self.rd_c = {}
        self.rd_d = {}
        self.dma_hist = {"sp": [], "pool": []}
        self.cc_last = None
        self.open_dmas = []
        self.pending = {}

    def add(self, eng, fn, reads=(), writes=(), kind="c", extra=(), nobar=False):
        i = len(self.ops)
        deps = set(extra)
        for r in reads:
            w = self.lastw.get(r)
            if w is not None:
                deps.add(w)
        for r in writes:
            w = self.lastw.get(r)
            if w is not None:
                deps.add(w)
            deps.update(self.rd_c.get(r, {}).values())
            deps.update(self.rd_d.get(r, ()))
        if kind == "d":
            h = self.dma_hist[eng]
            k = self.KRING[eng]
            if len(h) >= k:
                deps.add(h[-k])
            h.append(i)
            if not nobar:
                self.open_dmas.append(i)
        if kind == "cc":
            if self.cc_last is not None:
                deps.add(self.cc_last)
            self.cc_last = i
            self.open_dmas.append(i)
        if eng in self.pending and not nobar:
            deps.update(self.pending.pop(eng))
        self.ops.append(Op(eng, fn, deps, kind))
        for r in reads:
            if kind in ("d", "cc"):
                self.rd_d.setdefault(r, []).append(i)
            else:
                self.rd_c.setdefault(r, {})[eng] = i
        for r in writes:
            self.lastw[r] = i
            self.rd_c[r] = {}
            self.rd_d[r] = []
        return i

    def fence(self):
        ids = set(self.open_dmas)
        seen = set()
        for i in range(len(self.ops) - 1, -1, -1):
            e = self.ops[i].eng
            if e in ("pe", "act", "dve") and e not in seen and self.ops[i].kind in ("c", "bar"):
                seen.add(e)
                ids.add(i)
            if len(seen) == 3:
                break
        return ids

    def barrier(self):
        ids = []
        for e in ("pe", "act", "dve"):
            ids.append(self.add(e, lambda eng: eng.drain(), kind="bar"))
        allids = set(ids) | set(self.open_dmas)
        self.open_dmas = []
        for e in ("pe", "act", "dve", "pool", "sp"):
            self.pending[e] = set(allids)
        keep = lambda dct: {k: v for k, v in dct.items() if k[0] == "wbuf"}
        self.lastw, self.rd_c, self.rd_d = keep(self.lastw), keep(self.rd_c), keep(self.rd_d)

    def finish(self):
        self.barrier()
        for e in ("pe", "act", "dve"):
            self.add(e, lambda eng: eng.drain(), kind="bar")
        self.add("sp", None, kind="w")
        self.add("pool", None, kind="w")

    def emit(self, nc, block, sems, dsems, ccsem):
        ops = self.ops
        used = [False] * len(ops)
        for op in ops:
            for d in op.deps:
                used[d] = True
        cnt = {e: 0 for e in ("pe", "act", "dve", "pool")}
        didx = {"sp": 0, "pool": 0}
        ccn = 0
        for i, op in enumerate(ops):
            if op.kind in ("c", "bar"):
                if used[i] or op.kind == "bar":
                    cnt[op.eng] += 1
                    op.sig = (sems[op.eng], cnt[op.eng], 1, op.eng)
            elif op.kind == "d":
                j = didx[op.eng]
                didx[op.eng] += 1
                k = self.KRING[op.eng]
                op.sig = (dsems[op.eng][j % k], 16 * (j // k + 1), 16, None)
            elif op.kind == "cc":
                ccn += 1
                op.sig = (ccsem, ccn, 1, None)
        per = {e: [] for e in ("pe", "act", "dve", "pool", "sp")}
        for op in ops:
            per[op.eng].append(op)

        def run(engname, eng):
            waited = {}
            for op in per[engname]:
                for d in sorted(op.deps):
                    dop = ops[d]
                    if dop.sig is None:
                        continue
                    sem, val, _, src = dop.sig
                    if src == engname and (engname == "pe" or not DBG.get("sew", True)):
                        continue
                    key = id(sem)
                    if waited.get(key, 0) >= val:
                        continue
                    eng.wait_ge(sem, val)
                    waited[key] = val
                if op.fn is None:
                    continue
                ins = op.fn(eng)
                if op.sig is not None:
                    if op.kind == "cc":
                        ins.then_inc(op.sig[0])
                    else:
                        ins.then_inc(op.sig[0], op.sig[2])

        @block.tensor
        def _(e):
            run("pe", e)

        @block.scalar
        def _(e):
            run("act", e)

        @block.vector
        def _(e):
            run("dve", e)

        @block.gpsimd
        def _(e):
            run("pool", e)

        @block.sync
        def _(e):
            run("sp", e)


def build_program(stop_after=None):
    nc = bass.Bass("TRN2", target_bir_lowering=False)

    def din(name, shape, dt=F32):
        return nc.dram_tensor(name, list(shape), dt, kind="ExternalInput")

    def dout(name, shape, dt=F32):
        return nc.dram_tensor(name, list(shape), dt, kind="ExternalOutput")

    xin = din("xin", [TOK, D])
    pin = din("pin", [DEPTH, TOK, 256])
    sret = din("sret", [DEPTH, 16, 4, 256, 256])
    spool = din("spool", [DEPTH, 16, 15, 1024])
    w_in = din("w_in", [DEPTH, D, INC])
    w_pool = din("w_pool", [DEPTH, 4, 256, 256])
    pool_scale = din("pool_scale", [DEPTH, 1024])
    w_out = din("w_out", [DEPTH, D, D])
    gcol_d = din("gcol", [128, DEPTH * 16])
    post_g = din("post_g", [DEPTH, D])
    w_ple = din("w_ple", [DEPTH, 256, D])
    ple_g = din("ple_g", [DEPTH, D])
    w_gate = din("w_gate", [DEPTH, D, D])
    b_gate = din("b_gate", [DEPTH, D])
    cs_d = din("cs", [NT, 128, 256])
    dqk_d = din("dqk", [128, 16])
    masks_d = din("masks", [128, 2 * 128], BF16)
    amat_d = din("amat", [128, 16 * 128], BF16)
    ass_d = din("ass", [128, 4 * 64], BF16)
    ident_d = din("ident", [128, 128], BF16)
    bm8_d = din("bm8", [128, 64])
    flag_d = din("flag", [128, 1])

    yout = dout("yout", [TOK, D])
    nrp = dout("nrp", [DEPTH, 4, 256, 256])
    npp = dout("npp", [DEPTH, 15, 1024])
    nrs = dout("nrs", [DEPTH, 16, 4, 256, 256])
    nps = dout("nps", [DEPTH, 16, 15, 1024])

    xpark = nc.dram_tensor("xpark", [TOK, D], F32)
    bS_in = [[nc.dram_tensor(f"bSi{l}{h}", [256, 256], F32) for h in range(4)] for l in range(DEPTH)]
    bS_out = [[nc.dram_tensor(f"bSo{l}{h}", [512, 256], F32) for h in range(4)] for l in range(DEPTH)]
    bU_in = [[nc.dram_tensor(f"bUi{l}{g}", [128, 256], BF16) for g in range(4)] for l in range(DEPTH)]
    bU_out = [[nc.dram_tensor(f"bUo{l}{g}", [256, 256], BF16) for g in range(4)] for l in range(DEPTH)]
    PAIRS = DBG.get("pairs", [[0, 1], [2, 3], [4, 5], [6, 7]])

    S = Sched()
    ARENA_W = 53200
    from contextlib import ExitStack
    es = ExitStack()
    arena = es.enter_context(nc.sbuf_tensor("arena", [128, ARENA_W], F32))
    psb = [es.enter_context(nc.psum_tensor(f"psb{k}", [128, 512], F32)) for k in range(6)]
    ptb = [es.enter_context(nc.psum_tensor(f"ptb{k}", [128, 1024], BF16)) for k in range(2)]
    sem_names = ("pe", "act", "dve", "pool")
    sems = {e: es.enter_context(nc.semaphore(f"s_{e}")) for e in sem_names}
    dsems = {q: [es.enter_context(nc.semaphore(f"d_{q}{k}")) for k in range(Sched.KRING[q])] for q in ("sp", "pool")}
    ccsem = es.enter_context(nc.semaphore("ccsem"))

    def view(off, dt, *shape):
        n = int(np.prod(shape))
        nb = n * (4 if dt == F32 else 2)
        assert off % 4 == 0 and nb % 4 == 0 and off + nb <= ARENA_W * 4, (off, nb)
        a = arena[:, off // 4:(off + nb) // 4]
        if dt != F32:
            a = a.bitcast(dt)
        if len(shape) == 2:
            a = a.rearrange("p (a b) -> p a b", a=shape[0])
        elif len(shape) == 3:
            a = a.rearrange("p (a b c) -> p a b c", a=shape[0], b=shape[1])
        return a

    def mk(base, off, dims, parts=128):
        pst = base.ap[0][0]
        return bass.AP(base.tensor, base.offset + off, [[pst, parts]] + [list(d) for d in dims])

    R1, R2, R3, R4, R5 = 0, 36864, 73728, 98304, 172032
    hT = view(R1, BF16, 16, TOK)
    zb = view(R1, BF16, NT, D)
    mixT = view(R2, BF16, 16, TOK)
    wbuf = [view(R3 + 8192 * k, BF16, 16, 256) for k in range(3)]
    xbuf = view(R4, F32, NT, D)

    class Bump:
        def __init__(self, off, lim):
            self.off, self.lim = off, lim

        def __call__(self, dt, *shape):
            n = int(np.prod(shape)) * (4 if dt == F32 else 2)
            n = (n + 31) // 32 * 32
            v = view(self.off, dt, *shape)
            self.off += n
            assert self.off <= self.lim, (self.off, self.lim)
            return v

    p5 = Bump(R5, ARENA_W * 4)
    ident = p5(BF16, 128)
    masks = p5(BF16, 2, 128)
    amat = p5(BF16, 16, 128)
    assm = p5(BF16, 4, 64)
    dqk = p5(F32, 16)
    bm8 = p5(F32, 4, 16)
    flag = p5(F32, 8)
    gcol = p5(F32, DEPTH * 16)
    stats = p5(F32, 256)
    junk = p5(BF16, 512)
    gtab = p5(F32, 4, 256)
    wple = p5(BF16, 2, 2, 256)
    pbf = p5(BF16, NT, 256)
    pT = p5(BF16, 2, TOK)
    tmpf = p5(F32, 4, 256)
    csr = p5(F32, 2, 256)
    v8s = p5(BF16, 256)
    sgpr = p5(BF16, 2, 256)
    hb_off = p5.off
    hb = p5(BF16, 2, D)
    ssA = stats[:, 0:36]
    ssS = stats[:, 36:45]
    rsA = stats[:, 45:54]
    ssz = stats[:, 54:126]
    rsz = stats[:, 126:135]
    ssp = stats[:, 135:207]
    rsp = stats[:, 207:216]
    ssy = stats[:, 216:224]
    rsy = stats[:, 224:232]

    pb = Bump(R4, R4 + 73728)
    kT = pb(BF16, 2, TOK)
    ktok = pb(BF16, NT, 256)
    vall = pb(BF16, NT, 256)
    sgr = pb(BF16, NT, 256)
    uall = pb(BF16, NT, 256)
    qTall = uall.rearrange("p a b -> p (a b)").rearrange("p (a b) -> p a b", a=2)
    qm = pb(BF16, 2, 16, 128)
    km = pb(BF16, 16, 256)
    Sf = pb(F32, 2, 256)
    Sbf = pb(BF16, 2, 2, 256)
    Sin = pb(F32, 3, 2, 256)
    Sout = pb(F32, 2, 2, 256)
    Sbb = pb(BF16, 3, 2, 256)
    Uev = pb(F32, 2, 256)
    uf32 = pb(F32, 2, 256)
    uprev_raw = pb(BF16, 256)
    uprevX = pb(BF16, 256)
    bufS = pb(BF16, 2, 256)
    pooledT = pb(BF16, 2, 2, 128)
    T13 = pb(F32, 2, 256)
    T42 = pb(F32, 2, 256)
    qtok = pb(BF16, 2, 256)
    scm = pb(BF16, 2, 128)
    rety = pb(BF16, 2, 256)
    pooly = pb(BF16, 2, 256)
    kdr = pb(BF16, 2, 256)
    kUr = pb(BF16, 2, 256)
    pb2 = Bump(hb_off, hb_off + 8192)
    wpool = pb2(BF16, 8, 256)
    pst = pb2(F32, 1024)

    def psh(k):
        return psb[k // 2][:, (k % 2) * 256:(k % 2) * 256 + 256]

    class RR:
        def __init__(self, items):
            self.items, self.i = items, 0

        def nxt(self):
            v = self.items[self.i % len(self.items)]
            self.i += 1
            return v

    PJ = RR([0, 2, 4])
    PT = RR([0, 1])
    TMP = RR([0, 1, 2, 3])
    HB = RR([0, 1])
    RB = RR([0, 1])

    def res(*a):
        return a

    def BK(k):
        return ("bank", k // 2)

    def dma(q, out, in_, reads=(), writes=(), nobar=False):
        def fn(eng, out=out, in_=in_):
            return eng.dma_start(out=out, in_=in_)
        return S.add(q, fn, reads=reads, writes=writes, kind="d", nobar=nobar)

    def act(out, in_, func, reads, writes, scale=None, bias=None, accum=None, extra=()):
        def fn(eng):
            kw = {}
            if scale is not None:
                kw["scale"] = scale
            if bias is not None:
                kw["bias"] = bias
            if accum is not None:
                kw["accum_out"] = accum
            return eng.activation(out=out, in_=in_, func=func, **kw)
        return S.add("act", fn, reads=reads, writes=writes, extra=extra)

    def dve(fn, reads, writes, extra=()):
        return S.add("dve", fn, reads=reads, writes=writes, extra=extra)

    def tt(out, a, b, op, reads, writes):
        return dve(lambda e: e.tensor_tensor(out=out, in0=a, in1=b, op=op), reads, writes)

    def ts(out, a, s1, s2, op0, op1, reads, writes):
        if op1 is None:
            return dve(lambda e: e.tensor_scalar(out=out, in0=a, scalar1=s1, scalar2=None, op0=op0), reads, writes)
        return dve(lambda e: e.tensor_scalar(out=out, in0=a, scalar1=s1, scalar2=s2, op0=op0, op1=op1), reads, writes)

    def rsqrt(out, a, scale, bias, reads, writes):
        act(out, a, AF.Sqrt, reads, writes, scale=float(scale), bias=float(bias))
        dve(lambda e: e.reciprocal(out=out, in_=out), writes, writes)

    def stt(out, a, s, b, op0, op1, reads, writes):
        return dve(lambda e: e.scalar_tensor_tensor(out=out, in0=a, scalar=s, in1=b, op0=op0, op1=op1), reads, writes)

    def mm_group(out, pairs, reads, writes, first=True, last=True):
        def fn(eng):
            ins = None
            n = len(pairs)
            for k, (l, r) in enumerate(pairs):
                ins = eng.matmul(out, l, r, start=(first and k == 0), stop=(last and k == n - 1))
            return ins
        return S.add("pe", fn, reads=reads, writes=writes)

    def transposes(ptk, srcs, reads):
        def fn(eng):
            ins = None
            for k, s_ in enumerate(srcs):
                ins = eng.transpose(ptb[ptk][:, k * 128:(k + 1) * 128], s_, ident)
            return ins
        return S.add("pe", fn, reads=list(reads) + [res("ident")], writes=[res("pt", ptk)])

    def bcast_row(t, off, n):
        return bass.AP(t.ap().tensor, off, [[0, 128], [1, n]])

    def load_consts():
        dma("sp", ident, ident_d[:, :], writes=[res("ident")])
        dma("sp", masks, masks_d.ap().rearrange("p (a b) -> p a b", a=2), writes=[res("masks")])
        dma("sp", amat, amat_d.ap().rearrange("p (a b) -> p a b", a=16), writes=[res("amat")])
        dma("sp", assm, ass_d.ap().rearrange("p (a b) -> p a b", a=4), writes=[res("ass")])
        dma("sp", dqk, dqk_d[:, :], writes=[res("dqk")])
        dma("sp", bm8, bm8_d.ap().rearrange("p (a b) -> p a b", a=4), writes=[res("bm8")])
        dma("sp", flag[:, 0:1], flag_d[:, :], writes=[res("flag")])
        dma("sp", gcol, gcol_d[:, :], writes=[res("gcol")])

    wstate = {"issued": 0, "used": 0, "plan": []}

    def plan_block(src2d, c0, rows=D):
        wstate["plan"].append((src2d, c0, rows))

    def issue_upto(n):
        while wstate["issued"] < min(n, len(wstate["plan"])):
            src2d, c0, rows = wstate["plan"][wstate["issued"]]
            k = wstate["issued"] % 3
            wstate["issued"] += 1
            kc = rows // 128
            src = src2d.rearrange("(kc p) c -> p kc c", p=128)[:, :, c0:c0 + 256]
            dma("pool", wbuf[k][:, 0:kc, :], src, writes=[res("wbuf", k)], nobar=True)

    def load_block(src2d, c0, rows=D):
        n = wstate["used"]
        assert wstate["plan"][n][1] == c0 and wstate["plan"][n][0] is src2d, (n, c0)
        wstate["used"] += 1
        issue_upto(n + 3)
        return n % 3

    def phase_a_pre(l, i):
        x_i = xbuf[:, i, :]
        for s4 in range(4):
            act(junk, x_i[:, s4 * 512:(s4 + 1) * 512], AF.Square, reads=[res("x", i)], writes=[res("junk"), res("ssA", i)],
                accum=ssA[:, i * 4 + s4:i * 4 + s4 + 1])
        dve(lambda e: e.reduce_sum(out=ssS[:, i:i + 1], in_=ssA[:, i * 4:i * 4 + 4], axis=AX.X), [res("ssA", i)], [res("ssS", i)])
        rsqrt(rsA[:, i:i + 1], ssS[:, i:i + 1], 1.0 / D, EPS, [res("ssS", i)], [res("rsA", i)])
        b = HB.nxt()
        ts(hb[:, b, :], x_i, rsA[:, i:i + 1], None, ALU.mult, None, [res("x", i), res("rsA", i)], [res("hb", b)])
        return b

    def phase_a_pe(l, i, b, extra=()):
        for half in range(2):
            q = PT.nxt()
            transposes(q, [hb[:, b, (half * 8 + k) * 128:(half * 8 + k + 1) * 128] for k in range(8)], [res("hb", b)])
            gc = mk(gcol, l * 16 + half * 8, [[1, 8], [0, 128]])
            S.add("dve", (lambda e, half=half, q=q, gc=gc: e.tensor_tensor(
                out=hT[:, half * 8:half * 8 + 8, i * 128:(i + 1) * 128], in0=ptb[q][:, :].rearrange("p (a b) -> p a b", a=8), in1=gc, op=ALU.mult)),
                reads=[res("pt", q), res("gcol")], writes=[res("hT", i)], extra=extra)

    def phase_a(l, i):
        b = phase_a_pre(l, i)
        phase_a_pe(l, i, b)

    def proj(actT, actres, i, slot, nk=16):
        u = PJ.nxt()
        pairs = [(actT[:, kc, i * 128:(i + 1) * 128], wbuf[slot][:, kc, :]) for kc in range(nk)]
        mm_group(psh(u), pairs, [res(actres, i), res("wbuf", slot)], [BK(u)])
        return u

    def rope(u, i, l, h, is_k, dst):
        kind = 1 if i == 8 else 0
        col = (8 if is_k else 0) + h * 2 + kind
        sc = dqk[:, col:col + 1]
        b = RB.nxt()
        src = psh(u).rearrange("p (a b) -> p a b", a=2)
        cosb = mk(csr, (i % 2) * 256, [[0, 2], [1, 128]])
        sinb = mk(csr, (i % 2) * 256 + 128, [[0, 2], [1, 128]])
        t13 = T13[:, b, :].rearrange("p (a b) -> p a b", a=2)
        t42 = T42[:, b, :].rearrange("p (a b) -> p a b", a=2)
        stt(t13, src, sc, cosb, ALU.mult, ALU.mult, [BK(u), res("cs", i % 2), res("dqk")], [res("T13", b)])
        stt(t42, src, sc, sinb, ALU.mult, ALU.mult, [BK(u), res("cs", i % 2), res("dqk")], [res("T42", b)])
        return b

    def rope_fin(b, dst, dres):
        eng_ = "pool" if DBG.get("pool_rope", False) else "dve"
        S.add(eng_, lambda e: e.tensor_tensor(out=dst[:, 0:128], in0=T13[:, b, 0:128], in1=T42[:, b, 128:256], op=ALU.subtract),
              reads=[res("T13", b), res("T42", b)], writes=dres)
        S.add(eng_, lambda e: e.tensor_tensor(out=dst[:, 128:256], in0=T13[:, b, 128:256], in1=T42[:, b, 0:128], op=ALU.add),
              reads=[res("T13", b), res("T42", b)], writes=dres)

    def load_cs(i):
        dma("sp", csr[:, i % 2, :], cs_d[i, :, :], writes=[res("cs", i % 2)])

    def proj_split(actT, actres, i, slot):
        u = PJ.nxt()
        pairs = [(actT[:, kc, i * 128:(i + 1) * 128], wbuf[slot][:, kc, :]) for kc in range(16)]

        def pa():
            mm_group(psh(u), pairs[:8], [res(actres, i), res("wbuf", slot)], [BK(u)], first=True, last=False)

        def pb_():
            mm_group(psh(u), pairs[8:], [res(actres, i), res("wbuf", slot)], [BK(u)], first=False, last=True)
        return u, pa, pb_

    def k_pass(l, h, scan_h=None):
        slot = load_block(w_in_l[l], 3072 + 256 * h)
        prev = None
        for i in range(NT + 1):
            if i < NT:
                load_cs(i)
                u, pa, pb_ = proj_split(hT, "hT", i, slot)
                if scan_h is not None:
                    if i >= 1:
                        scan_part3a(l, scan_h, i - 1)
                    scan_part1(l, scan_h, i)
                pa()
                if scan_h is not None:
                    if i >= 1:
                        scan_part3b(l, scan_h, i - 1)
                    scan_part2(l, scan_h, i)
                pb_()
                b = rope(u, i, l, h, True, None)
                rope_fin(b, ktok[:, i, :], [res("ktok", i)])
            if prev is not None:
                j = prev
                q = PT.nxt()
                transposes(q, [ktok[:, j, 0:128], ktok[:, j, 128:256]], [res("ktok", j)])
                act(kT[:, :, j * 128:(j + 1) * 128], ptb[q][:, 0:256].rearrange("p (a b) -> p a b", a=2), AF.Copy,
                    [res("pt", q)], [res("kT", j)])
            prev = i if i < NT else None

    def v_pass(l, h, sample_h=None):
        slot = load_block(w_in_l[l], 4096 + 256 * h)
        spread = sample_h is not None
        for i in range(NT):
            u = proj(hT, "hT", i, slot)
            if spread and i == NT - 1:
                sample_end(l, sample_h)
            act(vall[:, i, :], psh(u), AF.Copy, [BK(u)], [res("vall", i)])
            if spread and i < 8:
                sample_batch(l, sample_h, 2 * i)
                sample_batch(l, sample_h, 2 * i + 1)
        if spread:
            scan_part3(l, sample_h, NT - 1)

    def u_exchange(l, h):
        g = GAM[h]
        UH = (10, 6)
        pairs0, pairs1 = [], []
        for c in range(8):
            b = c % 4
            kbuf = kUr[:, b, :] if b < 2 else kdr[:, b - 2, :]
            kres = res("kUr", b) if b < 2 else res("kdr", b - 2)
            act(kbuf, ktok[:, c, :], AF.Copy, [res("ktok", c)], [kres], scale=float(g ** (128 * (8 - c))))
            for kc2 in range(2):
                def fn(eng, c=c, kc2=kc2, kbuf=kbuf):
                    return eng.matmul(psh(UH[kc2]), kbuf[:, kc2 * 128:(kc2 + 1) * 128], vall[:, c, :], start=(c == 0), stop=(c == 7))
                S.add("pe", fn, reads=[kres, res("vall", c)], writes=[BK(UH[kc2])])
        for kc2 in range(2):
            act(Uev[:, kc2, :], psh(UH[kc2]), AF.Copy, [BK(UH[kc2])], [res("Uev", kc2)])
        dma("sp", bS_in[l][h].ap().rearrange("(kc p) e -> p kc e", p=128), Uev[:, :, :], reads=[res("Uev", 0), res("Uev", 1)], writes=[res("bSi", l, h)])

        def cc(eng):
            return eng.collective_compute("AllGather", ALU.bypass, replica_groups=PAIRS,
                                          ins=[bS_in[l][h].ap().opt()], outs=[bS_out[l][h].ap().opt()])
        S.add("pool", cc, reads=[res("bSi", l, h)], writes=[res("bSo", l, h)], kind="cc")

    def s_init(l, h):
        dma("sp", Sf[:, :, :], bS_out[l][h].ap()[0:256, :].rearrange("(kc p) e -> p kc e", p=128), reads=[res("bSo", l, h)], writes=[res("Sf")])
        ts(Sf[:, :, :], Sf[:, :, :], flag[:, 0:1], None, ALU.mult, None, [res("Sf"), res("flag")], [res("Sf")])
        act(Sbf[:, 0, :, :], Sf[:, :, :], AF.Copy, [res("Sf")], [res("Sbf", 0)])

    def gr_pass(l, h, sample_h=None):
        slot = load_block(w_in_l[l], 5120 + 256 * h)
        spread = sample_h is not None
        for i in range(NT):
            u = proj(hT, "hT", i, slot)
            if spread and i == NT - 1:
                sample_end(l, sample_h)
                scan_part3(l, sample_h, NT - 1)
            act(sgr[:, i, :], psh(u), AF.Silu, [BK(u)], [res("sgr", i)])
            if spread and i < 7:
                sample_batch(l, sample_h, 9 + i)

    QTU = {"ids": []}

    def qT_(c, kc):
        return qTall[:, kc, c * 128:(c + 1) * 128]

    def scan_part1(l, h, c):
        g = GAM[h]
        QTU["ids"].append(mm_group(psh(6)[:, 0:128], [(kT[:, kc, c * 128:(c + 1) * 128], qT_(c, kc)) for kc in range(2)],
                                   [res("kT", c), res("qTall", c)], [BK(6)]))
        sm = c % 2
        tt(scm[:, sm, :], psh(6)[:, 0:128], masks[:, 1 if c == 8 else 0, :], ALU.mult, [BK(6), res("masks")], [res("scm", sm)])
        if c < 8:
            kb = c % 2
            act(kdr[:, kb, :], ktok[:, c, :], AF.Copy, [res("ktok", c)], [res("kdr", kb)], scale=float(g ** 128))
        else:
            qm_dst = mk(qm, 0, [[16 * 128, 2], [136, 16], [1, 8]])
            q_src = mk(qTall, 8 * 128, [[TOK, 2], [8, 16], [1, 8]])
            QTU["ids"].append(dve(lambda e: e.tensor_copy(out=qm_dst, in_=q_src), [res("qTall", 8)], [res("qm")]))
            k_b = mk(ktok, 8 * 256, [[0, 16], [1, 256]])
            m_b = mk(bm8, h * 16, [[1, 16], [0, 256]])
            tt(km[:, :, :], k_b, m_b, ALU.mult, [res("ktok", 8), res("bm8")], [res("km")])

    def scan_part2(l, h, c):
        g = GAM[h]
        sm = c % 2
        if c < 8:
            sb_cur, sb_nxt = c % 2, (c + 1) % 2
            kb = c % 2
            pairs = [(scm[:, sm, :], vall[:, c, :])] + [(qT_(c, kc), Sbf[:, sb_cur, kc, :]) for kc in range(2)]
            QTU["ids"].append(mm_group(psh(8), pairs, [res("scm", sm), res("vall", c), res("qTall", c), res("Sbf", sb_cur)], [BK(8)]))
            for kc2 in range(2):
                mm_group(psh(10 + kc2), [(kdr[:, kb, kc2 * 128:(kc2 + 1) * 128], vall[:, c, :])], [res("kdr", kb), res("vall", c)], [BK(10 + kc2)])
            stt(Sf[:, :, :], Sf[:, :, :], float(g ** 128), psb[5][:, :].rearrange("p (a b) -> p a b", a=2), ALU.mult, ALU.add,
                [res("Sf"), BK(10), BK(11)], [res("Sf")])
            act(Sbf[:, sb_nxt, :, :], Sf[:, :, :], AF.Copy, [res("Sf")], [res("Sbf", sb_nxt)])
            if c == 7:
                dma("sp", nrp[l, h].rearrange("(kc p) e -> p kc e", p=128), Sf[:, :, :], reads=[res("Sf")])
        else:
            sample_begin(l, h)
            if SAMPLE_INLINE["on"]:
                for b in range(16):
                    sample_batch(l, h, b)
                sample_end(l, h)
            return
        y_finish(c)

    SAMPLE_INLINE = {"on": False}

    def y_finish(c):
        yb = c % 2
        act(junk[:, 0:256], psh(8), AF.Square, [BK(8)], [res("junk"), res("ssy", yb)], accum=ssy[:, yb:yb + 1])
        act(Uev[:, yb, :], psh(8), AF.Copy, [BK(8)], [res("Uev", yb)])
        act(rsy[:, yb:yb + 1], ssy[:, yb:yb + 1], AF.Sqrt, [res("ssy", yb)], [res("rsy", yb)], scale=1.0 / 256.0, bias=float(EPS))

    def sample_begin(l, h):
        c = 8
        sm = c % 2
        sview = sret[l].rearrange("b h (kc p) e -> b h p kc e", p=128)
        act(v8s[:, :], vall[:, c, :], AF.Copy, [res("vall", c)], [res("v8s")])
        S.add("pe", lambda eng: eng.matmul(psh(8), scm[:, sm, :], v8s[:, :], start=True, stop=False),
              reads=[res("scm", sm), res("v8s")], writes=[BK(8)])
        for b in range(3):
            dma("sp", Sin[:, b % 3, :, :], sview[b, h], writes=[res("Sin", b % 3)])

    def sample_batch(l, h, b):
        sb_ = b % 3
        act(Sbb[:, sb_, :, :], Sin[:, sb_, :, :], AF.Copy, [res("Sin", sb_)], [res("Sbb", sb_)])
        if b >= 1:
            sample_stage_b(l, h, b - 1)

    def sample_stage_b(l, h, b):
        g = GAM[h]
        NB = 16
        sview = sret[l].rearrange("b h (kc p) e -> b h p kc e", p=128)
        oview = nrs[l].rearrange("b h (kc p) e -> b h p kc e", p=128)
        sb_ = b % 3
        so_ = b % 2

        def yf(eng, b=b, sb_=sb_):
            ins = None
            for kc in range(2):
                ins = eng.matmul(psh(8), qm[:, kc, b, :], Sbb[:, sb_, kc, :], start=False, stop=(b == NB - 1 and kc == 1))
            return ins
        S.add("pe", yf, reads=[res("qm"), res("Sbb", sb_), BK(8)], writes=[BK(8)])
        bank = 5 if b % 2 == 0 else 3
        for kc2 in range(2):
            mm_group(psh(2 * bank + kc2), [(km[:, b, kc2 * 128:(kc2 + 1) * 128], v8s[:, :])], [res("km"), res("v8s")],
                     [BK(2 * bank + kc2)])
        stt(Sout[:, so_, :, :], Sin[:, sb_, :, :], float(g ** 8), psb[bank][:, :].rearrange("p (a b) -> p a b", a=2), ALU.mult, ALU.add,
            [res("Sin", sb_), BK(2 * bank), BK(2 * bank + 1)], [res("Sout", so_)])
        dma("sp", oview[b, h], Sout[:, so_, :, :], reads=[res("Sout", so_)])
        if b + 3 < NB:
            dma("sp", Sin[:, sb_, :, :], sview[b + 3, h], writes=[res("Sin", sb_)])

    def sample_end(l, h):
        sample_stage_b(l, h, 15)
        y_finish(8)

    def scan_part3(l, h, c):
        scan_part3a(l, h, c)
        scan_part3b(l, h, c)

    def scan_part3a(l, h, c):
        yb = c % 2
        dve(lambda e: e.reciprocal(out=rsy[:, yb:yb + 1], in_=rsy[:, yb:yb + 1]), [res("rsy", yb)], [res("rsy", yb)])
        stt(rety[:, yb, :], Uev[:, yb, :], rsy[:, yb:yb + 1], sgr[:, c, :], ALU.mult, ALU.mult,
            [res("Uev", yb), res("rsy", yb), res("sgr", c)], [res("rety", yb)])

    def scan_part3b(l, h, c):
        yb = c % 2
        q = PT.nxt()
        transposes(q, [rety[:, yb, 0:128], rety[:, yb, 128:256]], [res("rety", yb)])
        act(mixT[:, 8 + 2 * h:10 + 2 * h, c * 128:(c + 1) * 128], ptb[q][:, 0:256].rearrange("p (a b) -> p a b", a=2), AF.Copy,
            [res("pt", q)], [res("mixT", c)])

    def scan_alone(l, h):
        inline = not DBG.get("sample_in_c", True)
        SAMPLE_INLINE["on"] = inline
        for c in range(NT):
            if c >= 1:
                scan_part3a(l, h, c - 1)
            scan_part1(l, h, c)
            if c >= 1:
                scan_part3b(l, h, c - 1)
            scan_part2(l, h, c)
        if inline:
            scan_part3(l, h, NT - 1)
        SAMPLE_INLINE["on"] = False

    def q_pass(l, h, uall_users):
        QTU["ids"] = []
        s_init(l, h)
        slot = load_block(w_in_l[l], 2048 + 256 * h)
        prev = None
        for i in range(NT + 1):
            if i < NT:
                load_cs(i)
                u = proj(hT, "hT", i, slot)
                b = rope(u, i, l, h, False, None)
                qb = i % 2
                rope_fin(b, qtok[:, qb, :], [res("qtok", qb)])
            if prev is not None:
                j = prev
                qb2 = j % 2
                q = PT.nxt()
                transposes(q, [qtok[:, qb2, 0:128], qtok[:, qb2, 128:256]], [res("qtok", qb2)])
                act(qTall[:, :, j * 128:(j + 1) * 128], ptb[q][:, 0:256].rearrange("p (a b) -> p a b", a=2), AF.Copy, [res("pt", q)],
                    [res("qTall", j)], extra=uall_users)
            prev = i if i < NT else None

    def pool_prep(l):
        dma("pool", wpool[:, :, :], w_pool[l].rearrange("g (kc p) d -> p (g kc) d", p=128), writes=[res("wpool")])
        dma("sp", pst, bcast_row(pool_scale, l * 1024, 1024), writes=[res("pst")])
        w4 = mk(wpool, 0, [[512, 4], [256, 2], [1, 256]])
        p4 = mk(pst, 0, [[256, 4], [0, 2], [1, 256]])
        tt(w4, w4, p4, ALU.mult, [res("wpool"), res("pst")], [res("wpool")])
        dma("sp", nps[l][:, 0:7, :], spool[l][:, 8:15, :])

    UALLU = {"ids": []}

    def u_pass(l, g):
        UALLU["ids"] = []
        slot = load_block(w_in_l[l], 256 * g)
        for hf in range(2):
            src = spool[l][8 * hf:8 * hf + 8].rearrange("b r c -> (b r) c")[:, 256 * g:256 * g + 256]
            dma("pool", mk(bufS, hf * 256, [[1, 256]], parts=120), src, writes=[res("bufS", hf)])
        for i in [7, 0, 1, 2, 3, 4, 5, 6, 8]:
            u = proj(hT, "hT", i, slot)
            act(uall[:, i, :], psh(u), AF.Copy, [BK(u)], [res("uall", i)], extra=list(QTU["ids"]))
            if i >= 7:
                act(uf32[:, i - 7, :], psh(u), AF.Copy, [BK(u)], [res("uf32", i - 7)])
            if i == 7:
                dma("sp", npp[l][:, 256 * g:256 * g + 256], uf32[113:128, 0, :], reads=[res("uf32", 0)])
                UALLU["ids"].append(dma("sp", bU_in[l][g].ap(), uall[:, 7, :], reads=[res("uall", 7)], writes=[res("bUi", l, g)]))

                def cc(eng):
                    return eng.collective_compute("AllGather", ALU.bypass, replica_groups=PAIRS,
                                                  ins=[bU_in[l][g].ap().opt()], outs=[bU_out[l][g].ap().opt()])
                S.add("pool", cc, reads=[res("bUi", l, g)], writes=[res("bUo", l, g)], kind="cc")
            if i == 8:
                for b in range(16):
                    dma("sp", nps[l][b, 7:15, 256 * g:256 * g + 256], uf32[8 * b:8 * b + 8, 1, :], reads=[res("uf32", 1)])

    def gp_pass_pool(l, g):
        slot = load_block(w_in_l[l], 1024 + 256 * g)
        order = [1, 2, 3, 4, 5, 6, 7, 8, 0]
        prev = None
        for n in range(NT + 1):
            if n < NT:
                i = order[n]
                u = proj(hT, "hT", i, slot)
                sb_ = n % 2
                act(sgpr[:, sb_, :], psh(u), AF.Silu, [BK(u)], [res("sgpr", sb_)])
            if prev is not None:
                if order[prev] == 0:
                    dma("sp", uprev_raw, bU_out[l][g].ap()[0:128, :], reads=[res("bUo", l, g)], writes=[res("uprev_raw")])
                    ts(uprevX, uprev_raw, flag[:, 0:1], None, ALU.mult, None, [res("uprev_raw"), res("flag")], [res("uprevX")])
                pool_tile(l, g, order[prev], prev % 2)
            prev = n if n < NT else None
        pool_tile_t(l, g, *PTL["prev"])
        PTL["prev"] = None

    def pool_tile(l, g, i, sgb):
        pbuf = sgb
        if i == 0:
            kind = 2
        elif i == 8:
            kind = 3
        else:
            kind = 0
        for half in range(2):
            out = psh(6)[:, half * 128:(half + 1) * 128]
            pairs = [(uall[:, i, half * 128:(half + 1) * 128], amat[:, g * 4 + kind, :])]
            rd = [res("uall", i), res("amat")]
            if i == 0:
                pairs.append((uprevX[:, half * 128:(half + 1) * 128], amat[:, g * 4 + 1, :]))
                rd.append(res("uprevX"))
            elif i < 8:
                pairs.append((uall[:, i - 1, half * 128:(half + 1) * 128], amat[:, g * 4 + 1, :]))
                rd.append(res("uall", i - 1))
            if i < 8:
                UALLU["ids"].append(mm_group(out, pairs, rd, [BK(6)]))
            else:
                def fn(eng, half=half, out=out, pairs=pairs):
                    ins = None
                    for hf in range(2):
                        eng.matmul(out[:, hf * 64:(hf + 1) * 64], pairs[0][0], pairs[0][1][:, hf * 64:(hf + 1) * 64], start=True, stop=False)
                        ins = eng.matmul(out[:, hf * 64:(hf + 1) * 64], mk(bufS, hf * 256 + half * 128, [[1, 128]], parts=120),
                                         mk(assm, g * 64, [[1, 64]], parts=120), start=False, stop=True)
                    return ins
                UALLU["ids"].append(S.add("pe", fn, reads=rd + [res("bufS", 0), res("bufS", 1), res("ass")], writes=[BK(6)]))
        act(pooledT[:, pbuf, :, :], psh(6).rearrange("p (a b) -> p a b", a=2), AF.Copy, [BK(6)], [res("pooledT", pbuf)])
        if PTL["prev"] is not None:
            pool_tile_t(l, g, *PTL["prev"])
        mm_group(psh(8), [(pooledT[:, pbuf, half, :], wpool[:, g * 2 + half, :]) for half in range(2)],
                 [res("pooledT", pbuf), res("wpool")], [BK(8)])
        tt(pooly[:, pbuf, :], psh(8), sgpr[:, sgb, :], ALU.mult, [BK(8), res("sgpr", sgb)], [res("pooly", pbuf)])
        PTL["prev"] = (i, pbuf)

    PTL = {"prev": None}

    def pool_tile_t(l, g, i, pbuf):
        q = PT.nxt()
        transposes(q, [pooly[:, pbuf, 0:128], pooly[:, pbuf, 128:256]], [res("pooly", pbuf)])
        act(mixT[:, 2 * g:2 * g + 2, i * 128:(i + 1) * 128], ptb[q][:, 0:256].rearrange("p (a b) -> p a b", a=2), AF.Copy,
            [res("pt", q)], [res("mixT", i)])

    def phase_b(l):
        dve(lambda e: e.memset(qm[:, :, :, :], 0.0), [], [res("qm")])
        pool_prep(l)
        QTU["ids"] = []
        cut = DBG.get("cut", 10 ** 9)
        n = [0]

        def tick():
            n[0] += 1
            if n[0] >= cut:
                raise StopIteration
        for s_ in range(4):
            k_pass(l, s_, scan_h=(s_ - 1 if s_ >= 1 else None)); tick()
            v_pass(l, s_, sample_h=(s_ - 1 if s_ >= 1 else None)); tick()
            u_exchange(l, s_); tick()
            gr_pass(l, s_); tick()
            u_pass(l, s_); tick()
            gp_pass_pool(l, s_); tick()
            q_pass(l, s_, list(UALLU["ids"])); tick()
        scan_alone(l, 3)

    ZREAD = {"ids": []}

    def phase_c(l):
        xsrc = xin if l == 0 else xpark
        ride = DBG.get("sample_in_c", True)

        FX = {"ids": (), "n": 0}

        def load_x1(i, extra=()):
            S.add("sp", (lambda eng, i=i: eng.dma_start(out=xbuf[:, i, :], in_=xsrc[i * 128:(i + 1) * 128, :])),
                  writes=[res("x", i)], kind="d", extra=extra)

        def load_x(extra=()):
            for i in range(NT):
                load_x1(i, extra)
        if not ride:
            load_x()
        dma("pool", pbf[:, :, :], pin[l].rearrange("(i p) c -> p i c", p=128), writes=[res("pbf")])
        dma("sp", gtab[:, 0, :], bcast_row(post_g, l * D, 256), writes=[res("gtab", 0)])
        for j in range(8):
            slot = load_block(w_out_l[l], 256 * j)
            gb = j % 2
            if j + 1 < 8:
                dma("sp", gtab[:, (j + 1) % 2, :], bcast_row(post_g, l * D + 256 * (j + 1), 256), writes=[res("gtab", (j + 1) % 2)])
            for i in range(NT):
                if ride and j == 0 and i == NT - 1:
                    sample_end(l, 3)
                    scan_part3(l, 3, NT - 1)
                u = proj(mixT, "mixT", i, slot)
                sq = act(junk[:, 0:256], psh(u), AF.Square, [BK(u)], [res("junk"), res("ssz", i)], accum=ssz[:, i * 8 + j:i * 8 + j + 1])
                S.add("dve", (lambda e, i=i, j=j, u=u, gb=gb: e.tensor_tensor(out=zb[:, i, j * 256:(j + 1) * 256], in0=psh(u), in1=gtab[:, gb, :], op=ALU.mult)),
                      reads=[BK(u), res("gtab", gb)], writes=[res("z", i)], extra=[sq])
                if ride and j == 0 and i < 8:
                    sample_batch(l, 3, 2 * i)
                    sample_batch(l, 3, 2 * i + 1)
                if ride and 1 <= j <= 3 and i in (0, 3, 6):
                    load_x1(FX["n"], FX["ids"])
                    FX["n"] += 1
            if ride and j == 0:
                FX["ids"] = S.fence()
        if DBG.get("cd_merge", True):
            phase_d_pre(l)
        ZREAD["ids"] = []
        for i in range(NT):
            dve(lambda e, i=i: e.reduce_sum(out=rsz[:, i:i + 1], in_=ssz[:, i * 8:i * 8 + 8], axis=AX.X), [res("ssz", i)], [res("rsz", i)])
            rsqrt(rsz[:, i:i + 1], rsz[:, i:i + 1], 1.0 / D, EPS, [res("rsz", i)], [res("rsz", i)])
        for i in range(NT):
            ZREAD["ids"].append(stt(xbuf[:, i, :], zb[:, i, :], rsz[:, i:i + 1], xbuf[:, i, :], ALU.mult, ALU.add,
                                    [res("z", i), res("rsz", i), res("x", i)], [res("x", i)]))
            b = HB.nxt()
            act(hb[:, b, :], xbuf[:, i, :], AF.Copy, [res("x", i)], [res("hb", b)])
            for half in range(2):
                q = PT.nxt()
                transposes(q, [hb[:, b, (half * 8 + k) * 128:(half * 8 + k + 1) * 128] for k in range(8)], [res("hb", b)])
                act(mixT[:, half * 8:half * 8 + 8, i * 128:(i + 1) * 128], ptb[q][:, :].rearrange("p (a b) -> p a b", a=8), AF.Copy,
                    [res("pt", q)], [res("mixT", i)])

    PA = {"b": 0}

    def load_wple(l, idx):
        j = idx % 8
        wb = idx % 2
        src = w_ple[l].rearrange("(kc p) c -> p kc c", p=128)[:, :, 256 * j:256 * j + 256]
        dma("pool", wple[:, wb, :, :], src, writes=[res("wple", wb)])

    def phase_d_pre(l):
        for i in range(NT):
            q = PT.nxt()
            transposes(q, [pbf[:, i, 0:128], pbf[:, i, 128:256]], [res("pbf")])
            act(pT[:, :, i * 128:(i + 1) * 128], ptb[q][:, 0:256].rearrange("p (a b) -> p a b", a=2), AF.Copy, [res("pt", q)], [res("pT", i)])

        load_wple(l, 0)
        for j in range(8):
            load_wple(l, j + 1)
            wb = j % 2
            for i in range(NT):
                u = PJ.nxt()
                mm_group(psh(u), [(pT[:, kc, i * 128:(i + 1) * 128], wple[:, wb, kc, :]) for kc in range(2)],
                         [res("pT", i), res("wple", wb)], [BK(u)])
                act(junk[:, 0:256], psh(u), AF.Square, [BK(u)], [res("junk"), res("ssp", i)], accum=ssp[:, i * 8 + j:i * 8 + j + 1])
        for i in range(NT):
            dve(lambda e, i=i: e.reduce_sum(out=rsp[:, i:i + 1], in_=ssp[:, i * 8:i * 8 + 8], axis=AX.X), [res("ssp", i)], [res("rsp", i)])
            rsqrt(rsp[:, i:i + 1], rsp[:, i:i + 1], 1.0 / D, EPS, [res("rsp", i)], [res("rsp", i)])

    def phase_d(l):
        if not DBG.get("cd_merge", True):
            phase_d_pre(l)
        for j in range(8):
            if j < 7:
                load_wple(l, 8 + j + 1)
            slot = load_block(w_gate_l[l], 256 * j)
            wb = j % 2
            gb = (2 * j) % 4
            dma("sp", gtab[:, gb, :], bcast_row(b_gate, l * D + 256 * j, 256), writes=[res("gtab", gb)])
            dma("sp", gtab[:, gb + 1, :], bcast_row(ple_g, l * D + 256 * j, 256), writes=[res("gtab", gb + 1)])
            for i in range(NT):
                ua = proj(mixT, "mixT", i, slot)
                ub = PJ.nxt()
                mm_group(psh(ub), [(pT[:, kc, i * 128:(i + 1) * 128], wple[:, wb, kc, :]) for kc in range(2)],
                         [res("pT", i), res("wple", wb)], [BK(ub)])
                t1 = TMP.nxt()
                t2 = TMP.nxt()
                tt(tmpf[:, t1, :], psh(ua), gtab[:, gb, :], ALU.add, [BK(ua), res("gtab", gb)], [res("tmpf", t1)])
                act(tmpf[:, t1, :], tmpf[:, t1, :], AF.Sigmoid, [res("tmpf", t1)], [res("tmpf", t1)])
                stt(tmpf[:, t2, :], psh(ub), rsp[:, i:i + 1], gtab[:, gb + 1, :], ALU.mult, ALU.mult,
                    [BK(ub), res("rsp", i), res("gtab", gb + 1)], [res("tmpf", t2)])
                tt(tmpf[:, t1, :], tmpf[:, t1, :], tmpf[:, t2, :], ALU.mult, [res("tmpf", t1), res("tmpf", t2)], [res("tmpf", t1)])
                tt(xbuf[:, i, j * 256:(j + 1) * 256], xbuf[:, i, j * 256:(j + 1) * 256], tmpf[:, t1, :], ALU.add,
                   [res("x", i), res("tmpf", t1)], [res("x", i)])
                if j == 7:
                    if l == 0:
                        if i >= 1:
                            phase_a_pe(1, i - 1, PA["b"], extra=list(ZREAD["ids"]))
                        PA["b"] = phase_a_pre(1, i)
                        dma("sp", xpark[i * 128:(i + 1) * 128, :], xbuf[:, i, :], reads=[res("x", i)], writes=[res("xpark", i)])
                        if i == NT - 1:
                            phase_a_pe(1, i, PA["b"], extra=list(ZREAD["ids"]))
                    else:
                        dma("sp", yout[i * 128:(i + 1) * 128, :], xbuf[:, i, :], reads=[res("x", i)])

    w_in_l = [w_in[l] for l in range(DEPTH)]
    w_out_l = [w_out[l] for l in range(DEPTH)]
    w_gate_l = [w_gate[l] for l in range(DEPTH)]
    for l in range(DEPTH):
        for s_ in range(4):
            for c0 in (3072 + 256 * s_, 4096 + 256 * s_, 5120 + 256 * s_, 256 * s_, 1024 + 256 * s_, 2048 + 256 * s_):
                plan_block(w_in_l[l], c0)
        for j in range(8):
            plan_block(w_out_l[l], 256 * j)
        for j in range(8):
            plan_block(w_gate_l[l], 256 * j)
    load_consts()
    for i in range(NT):
        dma("sp", xbuf[:, i, :], xin[i * 128:(i + 1) * 128, :], writes=[res("x", i)])
    for i in range(NT):
        if stop_after != "consts":
            phase_a(0, i)
    stages = []
    for l in range(DEPTH):
        stages += [("B", l), ("C", l), ("D", l)]
    if stop_after in ("consts", "A"):
        stages = None
    for (ph, l) in (stages or []):
        if not (ph == "C" and DBG.get("sample_in_c", True)) and not (ph == "D" and DBG.get("cd_merge", True)):
            S.barrier()
        if ph == "B":
            try:
                phase_b(l)
            except StopIteration:
                break
        elif ph == "C":
            phase_c(l)
        else:
            phase_d(l)
        if stop_after == (ph, l):
            break
    S.finish()

    with nc.Block() as block:
        S.emit(nc, block, sems, dsems, ccsem)
    es.close()
    return nc


def _bf(a):
    return np.ascontiguousarray(a.astype(ml_dtypes.bfloat16))


def _tables(core):
    odd = core % 2
    start = 1024 * odd
    half = 128
    inv = (np.float32(10000.0) ** (-(np.arange(half, dtype=np.float32)) / np.float32(half))).astype(np.float32)
    cs = np.zeros((NT, 128, 256), np.float32)
    p = np.arange(128)
    for i in range(NT):
        pos = (start + 128 * i + p) if i < 8 else (16384 + (p % 8))
        ang = (pos.astype(np.float32)[:, None] * inv[None, :]).astype(np.float32)
        cs[i, :, :128] = np.cos(ang.astype(np.float64))
        cs[i, :, 128:] = np.sin(ang.astype(np.float64))
    dqk = np.zeros((128, 16), np.float64)
    bm8 = np.zeros((128, 4, 16), np.float64)
    for h in range(4):
        g = GAM[h]
        dqk[:, h * 2 + 0] = g ** (p + 1.0)
        dqk[:, h * 2 + 1] = g ** ((p % 8) + 1.0)
        dqk[:, 8 + h * 2 + 0] = g ** (-(p + 1.0)) / 16.0
        dqk[:, 8 + h * 2 + 1] = g ** (-((p % 8) + 1.0)) / 16.0
        for b in range(16):
            bm8[:, h, b] = (p // 8 == b) * (g ** 8)
    masks = np.zeros((128, 2, 128), np.float32)
    jj, ii = np.meshgrid(p, p, indexing="ij")
    masks[:, 0, :] = (ii >= jj)
    masks[:, 1, :] = (ii >= jj) & (ii // 8 == jj // 8)
    amat = np.zeros((128, 4, 4, 128), np.float64)
    ass = np.zeros((128, 4, 64), np.float64)
    ss, tt_ = np.meshgrid(p, p, indexing="ij")
    for g, w in enumerate(WINS):
        inwin = (ss <= tt_) & (ss >= tt_ - w + 1)
        amat[:, g, 0, :] = inwin / w - (ss == tt_)
        amat[:, g, 1, :] = ((ss - 128) >= (tt_ - w + 1)) / w
        if odd:
            amat[:, g, 2, :] = amat[:, g, 0, :]
        else:
            cnt = np.minimum(tt_ + 1, w)
            amat[:, g, 2, :] = inwin / cnt - (ss == tt_)
        sb_, st_ = ss // 8, ss % 8
        tb_, ttt = tt_ // 8, tt_ % 8
        inw = (st_ <= ttt) & (st_ >= ttt - w + 1) & (sb_ == tb_)
        amat[:, g, 3, :] = inw / w - (ss == tt_)
        for r in range(120):
            b, rr = r // 15, r % 15
            for t in range(8):
                if rr >= 16 + t - w:
                    ass[r, g, b * 8 + t] = 1.0 / w
    return dict(
        cs=cs, dqk=dqk.astype(np.float32), bm8=bm8.reshape(128, 64).astype(np.float32),
        masks=_bf(masks.reshape(128, 256)), amat=_bf(amat.reshape(128, 2048)), ass=_bf(ass.reshape(128, 256)),
        ident=_bf(np.eye(128, dtype=np.float32)), flag=np.full((128, 1), float(odd), np.float32),
    )


_NC_CACHE = {}


def kernel(x_prompt, x_sample, state_ret, state_pool, p_prompt, p_sample,
           w_in, w_pool, pool_scale, w_out, pre_g, post_g, w_ple, ple_g, w_ple_gate, b_ple_gate, _stop_after=None):
    f = lambda a: np.ascontiguousarray(np.asarray(a, dtype=np.float32))
    x_prompt, x_sample, state_ret, state_pool, p_prompt, p_sample = map(f, (x_prompt, x_sample, state_ret, state_pool, p_prompt, p_sample))
    w_in, w_pool, pool_scale, w_out, pre_g, post_g, w_ple, ple_g, w_ple_gate, b_ple_gate = map(
        f, (w_in, w_pool, pool_scale, w_out, pre_g, post_g, w_ple, ple_g, w_ple_gate, b_ple_gate))
    key = _stop_after
    if key not in _NC_CACHE:
        _NC_CACHE[key] = build_program(_stop_after)
    nc = _NC_CACHE[key]
    gcol = np.ascontiguousarray(pre_g.reshape(DEPTH, 16, 128).transpose(2, 0, 1).reshape(128, DEPTH * 16))
    in_maps = []
    for c in range(NCORE):
        seq, hf = c // 2, c % 2
        xin = np.concatenate([x_prompt[seq, hf * 1024:(hf + 1) * 1024], x_sample[16 * c:16 * c + 16].reshape(128, D)], axis=0)
        pin = np.concatenate([p_prompt[:, seq, hf * 1024:(hf + 1) * 1024], p_sample[:, 16 * c:16 * c + 16].reshape(DEPTH, 128, 256)], axis=1)
        m = dict(xin=np.ascontiguousarray(xin), pin=np.ascontiguousarray(pin),
                 sret=np.ascontiguousarray(state_ret[:, 16 * c:16 * c + 16]), spool=np.ascontiguousarray(state_pool[:, 16 * c:16 * c + 16]),
                 w_in=w_in, w_pool=w_pool, pool_scale=pool_scale, w_out=w_out, gcol=gcol, post_g=post_g, w_ple=w_ple, ple_g=ple_g,
                 w_gate=w_ple_gate, b_gate=b_ple_gate)
        m.update(_tables(c))
        in_maps.append(m)
    if DBG.get("return_maps"):
        return nc, in_maps
    res_ = run_bass_kernel_spmd(nc, in_maps, core_ids=list(range(NCORE)))
    R = res_.results
    y_prompt = np.zeros((4, 2048, D), np.float32)
    y_sample = np.zeros((128, 8, D), np.float32)
    nrp = np.zeros((DEPTH, 4, 4, 256, 256), np.float32)
    npp = np.zeros((DEPTH, 4, 15, 1024), np.float32)
    nrs = np.zeros((DEPTH, 128, 4, 256, 256), np.float32)
    nps = np.zeros((DEPTH, 128, 15, 1024), np.float32)
    for c in range(NCORE):
        seq, hf = c // 2, c % 2
        yo = np.asarray(R[c]["yout"])
        y_prompt[seq, hf * 1024:(hf + 1) * 1024] = yo[:1024]
        y_sample[16 * c:16 * c + 16] = yo[1024:].reshape(16, 8, D)
        if hf == 1:
            nrp[:, seq] = np.asarray(R[c]["nrp"])
            npp[:, seq] = np.asarray(R[c]["npp"])
        nrs[:, 16 * c:16 * c + 16] = np.asarray(R[c]["nrs"])
        nps[:, 16 * c:16 * c + 16] = np.asarray(R[c]["nps"])
    return (y_prompt, y_sample, nrp, npp, nrs, nps)
```

```python
import numpy as np
import ml_dtypes
import concourse.bass as bass
import concourse.mybir as mybir
from concourse.bass_utils import run_bass_kernel_spmd

F32 = mybir.dt.float32
BF16 = mybir.dt.bfloat16
ALU = mybir.AluOpType
AF = mybir.ActivationFunctionType
AX = mybir.AxisListType

D = 2048
NT = 9
TOK = NT * 128
INC = 6144
DEPTH = 2
EPS = 1e-6
GAM = [1.0 - 2.0 ** (-5.0 - h) for h in range(4)]
WINS = (2, 4, 8, 16)
NCORE = 8
DBG = {}
SAME_ENGINE_WAITS = DBG.get("sew", True)


class Op:
    __slots__ = ("eng", "fn", "deps", "kind", "sig")

    def __init__(self, eng, fn, deps, kind):
        self.eng, self.fn, self.deps, self.kind, self.sig = eng, fn, deps, kind, None


class Sched:
    KRING = {"sp": 12, "pool": 6}

    def __init__(self):
        self.ops = []
        self.lastw = {}
        self.rd_c = {}
        self.rd_d = {}
        self.dma_hist = {"sp": [], "pool": []}
        self.cc_last = None
        self.open_dmas = []
        self.pending = {}

    def add(self, eng, fn, reads=(), writes=(), kind="c", extra=(), nobar=False):
        i = len(self.ops)
        deps = set(extra)
        for r in reads:
            w = self.lastw.get(r)
            if w is not None:
                deps.add(w)
        for r in writes:
            w = self.lastw.get(r)
            if w is not None:
                deps.add(w)
            deps.update(self.rd_c.get(r, {}).values())
            deps.update(self.rd_d.get(r, ()))
        if kind == "d":
            h = self.dma_hist[eng]
            k = self.KRING[eng]
            if len(h) >= k:
                deps.add(h[-k])
            h.append(i)
            if not nobar:
                self.open_dmas.append(i)
        if kind == "cc":
            if self.cc_last is not None:
                deps.add(self.cc_last)
            self.cc_last = i
            self.open_dmas.append(i)
        if eng in self.pending and not nobar:
            deps.update(self.pending.pop(eng))
        self.ops.append(Op(eng, fn, deps, kind))
        for r in reads:
            if kind in ("d", "cc"):
                self.rd_d.setdefault(r, []).append(i)
            else:
                self.rd_c.setdefault(r, {})[eng] = i
        for r in writes:
            self.lastw[r] = i
            self.rd_c[r] = {}
            self.rd_d[r] = []
        return i

    def fence(self):
        ids = set(self.open_dmas)
        seen = set()
        for i in range(len(self.ops) - 1, -1, -1):
            e = self.ops[i].eng
            if e in ("pe", "act", "dve") and e not in seen and self.ops[i].kind in ("c", "bar"):
                seen.add(e)
                ids.add(i)
            if len(seen) == 3:
                break
        return ids

    def barrier(self):
        ids = []
        for e in ("pe", "act", "dve"):
            ids.append(self.add(e, lambda eng: eng.drain(), kind="bar"))
        allids = set(ids) | set(self.open_dmas)
        self.open_dmas = []
        for e in ("pe", "act", "dve", "pool", "sp"):
            self.pending[e] = set(allids)
        keep = lambda dct: {k: v for k, v in dct.items() if k[0] == "wbuf"}
        self.lastw, self.rd_c, self.rd_d = keep(self.lastw), keep(self.rd_c), keep(self.rd_d)

    def finish(self):
        self.barrier()
        for e in ("pe", "act", "dve"):
            self.add(e, lambda eng: eng.drain(), kind="bar")
        self.add("sp", None, kind="w")
        self.add("pool", None, kind="w")

    def emit(self, nc, block, sems, dsems, ccsem):
        ops = self.ops
        used = [False] * len(ops)
        for op in ops:
            for d in op.deps:
                used[d] = True
        cnt = {e: 0 for e in ("pe", "act", "dve", "pool")}
        didx = {"sp": 0, "pool": 0}
        ccn = 0
        for i, op in enumerate(ops):
            if op.kind in ("c", "bar"):
                if used[i] or op.kind == "bar":
                    cnt[op.eng] += 1
                    op.sig = (sems[op.eng], cnt[op.eng], 1, op.eng)
            elif op.kind == "d":
                j = didx[op.eng]
                didx[op.eng] += 1
                k = self.KRING[op.eng]
                op.sig = (dsems[op.eng][j % k], 16 * (j // k + 1), 16, None)
            elif op.kind == "cc":
                ccn += 1
                op.sig = (ccsem, ccn, 1, None)
        per = {e: [] for e in ("pe", "act", "dve", "pool", "sp")}
        for op in ops:
            per[op.eng].append(op)

        def run(engname, eng):
            waited = {}
            for op in per[engname]:
                for d in sorted(op.deps):
                    dop = ops[d]
                    if dop.sig is None:
                        continue
                    sem, val, _, src = dop.sig
                    if src == engname and (engname == "pe" or not DBG.get("sew", True)):
                        continue
                    key = id(sem)
                    if waited.get(key, 0) >= val:
                        continue
                    eng.wait_ge(sem, val)
                    waited[key] = val
                if op.fn is None:
                    continue
                ins = op.fn(eng)
                if op.sig is not None:
                    if op.kind == "cc":
                        ins.then_inc(op.sig[0])
                    else:
                        ins.then_inc(op.sig[0], op.sig[2])

        @block.tensor
        def _(e):
            run("pe", e)

        @block.scalar
        def _(e):
            run("act", e)

        @block.vector
        def _(e):
            run("dve", e)

        @block.gpsimd
        def _(e):
            run("pool", e)

        @block.sync
        def _(e):
            run("sp", e)


def build_program(stop_after=None):
    nc = bass.Bass("TRN2", target_bir_lowering=False)

    def din(name, shape, dt=F32):
        return nc.dram_tensor(name, list(shape), dt, kind="ExternalInput")

    def dout(name, shape, dt=F32):
        return nc.dram_tensor(name, list(shape), dt, kind="ExternalOutput")

    xin = din("xin", [TOK, D])
    pin = din("pin", [DEPTH, TOK, 256])
    sret = din("sret", [DEPTH, 16, 4, 256, 256])
    spool = din("spool", [DEPTH, 16, 15, 1024])
    w_in = din("w_in", [DEPTH, D, INC])
    w_pool = din("w_pool", [DEPTH, 4, 256, 256])
    pool_scale = din("pool_scale", [DEPTH, 1024])
    w_out = din("w_out", [DEPTH, D, D])
    gcol_d = din("gcol", [128, DEPTH * 16])
    post_g = din("post_g", [DEPTH, D])
    w_ple = din("w_ple", [DEPTH, 256, D])
    ple_g = din("ple_g", [DEPTH, D])
    w_gate = din("w_gate", [DEPTH, D, D])
    b_gate = din("b_gate", [DEPTH, D])
    cs_d = din("cs", [NT, 128, 256])
    dqk_d = din("dqk", [128, 16])
    masks_d = din("masks", [128, 2 * 128], BF16)
    amat_d = din("amat", [128, 16 * 128], BF16)
    ass_d = din("ass", [128, 4 * 64], BF16)
    ident_d = din("ident", [128, 128], BF16)
    bm8_d = din("bm8", [128, 64])
    flag_d = din("flag", [128, 1])

    yout = dout("yout", [TOK, D])
    nrp = dout("nrp", [DEPTH, 4, 256, 256])
    npp = dout("npp", [DEPTH, 15, 1024])
    nrs = dout("nrs", [DEPTH, 16, 4, 256, 256])
    nps = dout("nps", [DEPTH, 16, 15, 1024])

    xpark = nc.dram_tensor("xpark", [TOK, D], F32)
    bS_in = [[nc.dram_tensor(f"bSi{l}{h}", [256, 256], F32) for h in range(4)] for l in range(DEPTH)]
    bS_out = [[nc.dram_tensor(f"bSo{l}{h}", [512, 256], F32) for h in range(4)] for l in range(DEPTH)]
    bU_in = [[nc.dram_tensor(f"bUi{l}{g}", [128, 256], BF16) for g in range(4)] for l in range(DEPTH)]
    bU_out = [[nc.dram_tensor(f"bUo{l}{g}", [256, 256], BF16) for g in range(4)] for l in range(DEPTH)]
    PAIRS = DBG.get("pairs", [[0, 1], [2, 3], [4, 5], [6, 7]])

    S = Sched()
    ARENA_W = 53200
    from contextlib import ExitStack
    es = ExitStack()
    arena = es.enter_context(nc.sbuf_tensor("arena", [128, ARENA_W], F32))
    psb = [es.enter_context(nc.psum_tensor(f"psb{k}", [128, 512], F32)) for k in range(6)]
    ptb = [es.enter_context(nc.psum_tensor(f"ptb{k}", [128, 1024], BF16)) for k in range(2)]
    sem_names = ("pe", "act", "dve", "pool")
    sems = {e: es.enter_context(nc.semaphore(f"s_{e}")) for e in sem_names}
    dsems = {q: [es.enter_context(nc.semaphore(f"d_{q}{k}")) for k in range(Sched.KRING[q])] for q in ("sp", "pool")}
    ccsem = es.enter_context(nc.semaphore("ccsem"))

    def view(off, dt, *shape):
        n = int(np.prod(shape))
        nb = n * (4 if dt == F32 else 2)
        assert off % 4 == 0 and nb % 4 == 0 and off + nb <= ARENA_W * 4, (off, nb)
        a = arena[:, off // 4:(off + nb) // 4]
        if dt != F32:
            a = a.bitcast(dt)
        if len(shape) == 2:
            a = a.rearrange("p (a b) -> p a b", a=shape[0])
        elif len(shape) == 3:
            a = a.rearrange("p (a b c) -> p a b c", a=shape[0], b=shape[1])
        return a

    def mk(base, off, dims, parts=128):
        pst = base.ap[0][0]
        return bass.AP(base.tensor, base.offset + off, [[pst, parts]] + [list(d) for d in dims])

    R1, R2, R3, R4, R5 = 0, 36864, 73728, 98304, 172032
    hT = view(R1, BF16, 16, TOK)
    zb = view(R1, BF16, NT, D)
    mixT = view(R2, BF16, 16, TOK)
    wbuf = [view(R3 + 8192 * k, BF16, 16, 256) for k in range(3)]
    xbuf = view(R4, F32, NT, D)

    class Bump:
        def __init__(self, off, lim):
            self.off, self.lim = off, lim

        def __call__(self, dt, *shape):
            n = int(np.prod(shape)) * (4 if dt == F32 else 2)
            n = (n + 31) // 32 * 32
            v = view(self.off, dt, *shape)
            self.off += n
            assert self.off <= self.lim, (self.off, self.lim)
            return v

    p5 = Bump(R5, ARENA_W * 4)
    ident = p5(BF16, 128)
    masks = p5(BF16, 2, 128)
    amat = p5(BF16, 16, 128)
    assm = p5(BF16, 4, 64)
    dqk = p5(F32, 16)
    bm8 = p5(F32, 4, 16)
    flag = p5(F32, 8)
    gcol = p5(F32, DEPTH * 16)
    stats = p5(F32, 256)
    junk = p5(BF16, 512)
    gtab = p5(F32, 4, 256)
    wple = p5(BF16, 2, 2, 256)
    pbf = p5(BF16, NT, 256)
    pT = p5(BF16, 2, TOK)
    tmpf = p5(F32, 4, 256)
    csr = p5(F32, 2, 256)
    v8s = p5(BF16, 256)
    sgpr = p5(BF16, 2, 256)
    hb_off = p5.off
    hb = p5(BF16, 2, D)
    ssA = stats[:, 0:36]
    ssS = stats[:, 36:45]
    rsA = stats[:, 45:54]
    ssz = stats[:, 54:126]
    rsz = stats[:, 126:135]
    ssp = stats[:, 135:207]
    rsp = stats[:, 207:216]
    ssy = stats[:, 216:224]
    rsy = stats[:, 224:232]

    pb = Bump(R4, R4 + 73728)
    kT = pb(BF16, 2, TOK)
    ktok = pb(BF16, NT, 256)
    vall = pb(BF16, NT, 256)
    sgr = pb(BF16, NT, 256)
    uall = pb(BF16, NT, 256)
    qTall = uall.rearrange("p a b -> p (a b)").rearrange("p (a b) -> p a b", a=2)
    qm = pb(BF16, 2, 16, 128)
    km = pb(BF16, 16, 256)
    Sf = pb(F32, 2, 256)
    Sbf = pb(BF16, 2, 2, 256)
    Sin = pb(F32, 3, 2, 256)
    Sout = pb(F32, 2, 2, 256)
    Sbb = pb(BF16, 3, 2, 256)
    Uev = pb(F32, 2, 256)
    uf32 = pb(F32, 2, 256)
    uprev_raw = pb(BF16, 256)
    uprevX = pb(BF16, 256)
    bufS = pb(BF16, 2, 256)
    pooledT = pb(BF16, 2, 2, 128)
    T13 = pb(F32, 2, 256)
    T42 = pb(F32, 2, 256)
    qtok = pb(BF16, 2, 256)
    scm = pb(BF16, 2, 128)
    rety = pb(BF16, 2, 256)
    pooly = pb(BF16, 2, 256)
    kdr = pb(BF16, 2, 256)
    kUr = pb(BF16, 2, 256)
    pb2 = Bump(hb_off, hb_off + 8192)
    wpool = pb2(BF16, 8, 256)
    pst = pb2(F32, 1024)

    def psh(k):
        return psb[k // 2][:, (k % 2) * 256:(k % 2) * 256 + 256]

    class RR:
        def __init__(self, items):
            self.items, self.i = items, 0

        def nxt(self):
            v = self.items[self.i % len(self.items)]
            self.i += 1
            return v

    PJ = RR([0, 2, 4])
    PT = RR([0, 1])
    TMP = RR([0, 1, 2, 3])
    HB = RR([0, 1])
    RB = RR([0, 1])

    def res(*a):
        return a

    def BK(k):
        return ("bank", k // 2)

    def dma(q, out, in_, reads=(), writes=(), nobar=False):
        def fn(eng, out=out, in_=in_):
            return eng.dma_start(out=out, in_=in_)
        return S.add(q, fn, reads=reads, writes=writes, kind="d", nobar=nobar)

    def act(out, in_, func, reads, writes, scale=None, bias=None, accum=None, extra=()):
        def fn(eng):
            kw = {}
            if scale is not None:
                kw["scale"] = scale
            if bias is not None:
                kw["bias"] = bias
            if accum is not None:
                kw["accum_out"] = accum
            return eng.activation(out=out, in_=in_, func=func, **kw)
        return S.add("act", fn, reads=reads, writes=writes, extra=extra)

    def dve(fn, reads, writes, extra=()):
        return S.add("dve", fn, reads=reads, writes=writes, extra=extra)

    def tt(out, a, b, op, reads, writes):
        return dve(lambda e: e.tensor_tensor(out=out, in0=a, in1=b, op=op), reads, writes)

    def ts(out, a, s1, s2, op0, op1, reads, writes):
        if op1 is None:
            return dve(lambda e: e.tensor_scalar(out=out, in0=a, scalar1=s1, scalar2=None, op0=op0), reads, writes)
        return dve(lambda e: e.tensor_scalar(out=out, in0=a, scalar1=s1, scalar2=s2, op0=op0, op1=op1), reads, writes)

    def rsqrt(out, a, scale, bias, reads, writes):
        act(out, a, AF.Sqrt, reads, writes, scale=float(scale), bias=float(bias))
        dve(lambda e: e.reciprocal(out=out, in_=out), writes, writes)

    def stt(out, a, s, b, op0, op1, reads, writes):
        return dve(lambda e: e.scalar_tensor_tensor(out=out, in0=a, scalar=s, in1=b, op0=op0, op1=op1), reads, writes)

    def mm_group(out, pairs, reads, writes, first=True, last=True):
        def fn(eng):
            ins = None
            n = len(pairs)
            for k, (l, r) in enumerate(pairs):
                ins = eng.matmul(out, l, r, start=(first and k == 0), stop=(last and k == n - 1))
            return ins
        return S.add("pe", fn, reads=reads, writes=writes)

    def transposes(ptk, srcs, reads):
        def fn(eng):
            ins = None
            for k, s_ in enumerate(srcs):
                ins = eng.transpose(ptb[ptk][:, k * 128:(k + 1) * 128], s_, ident)
            return ins
        return S.add("pe", fn, reads=list(reads) + [res("ident")], writes=[res("pt", ptk)])

    def bcast_row(t, off, n):
        return bass.AP(t.ap().tensor, off, [[0, 128], [1, n]])

    def load_consts():
        dma("sp", ident, ident_d[:, :], writes=[res("ident")])
        dma("sp", masks, masks_d.ap().rearrange("p (a b) -> p a b", a=2), writes=[res("masks")])
        dma("sp", amat, amat_d.ap().rearrange("p (a b) -> p a b", a=16), writes=[res("amat")])
        dma("sp", assm, ass_d.ap().rearrange("p (a b) -> p a b", a=4), writes=[res("ass")])
        dma("sp", dqk, dqk_d[:, :], writes=[res("dqk")])
        dma("sp", bm8, bm8_d.ap().rearrange("p (a b) -> p a b", a=4), writes=[res("bm8")])
        dma("sp", flag[:, 0:1], flag_d[:, :], writes=[res("flag")])
        dma("sp", gcol, gcol_d[:, :], writes=[res("gcol")])

    wstate = {"issued": 0, "used": 0, "plan": []}

    def plan_block(src2d, c0, rows=D):
        wstate["plan"].append((src2d, c0, rows))

    def issue_upto(n):
        while wstate["issued"] < min(n, len(wstate["plan"])):
            src2d, c0, rows = wstate["plan"][wstate["issued"]]
            k = wstate["issued"] % 3
            wstate["issued"] += 1
            kc = rows // 128
            src = src2d.rearrange("(kc p) c -> p kc c", p=128)[:, :, c0:c0 + 256]
            dma("pool", wbuf[k][:, 0:kc, :], src, writes=[res("wbuf", k)], nobar=True)

    def load_block(src2d, c0, rows=D):
        n = wstate["used"]
        assert wstate["plan"][n][1] == c0 and wstate["plan"][n][0] is src2d, (n, c0)
        wstate["used"] += 1
        issue_upto(n + 3)
        return n % 3

    def phase_a_pre(l, i):
        x_i = xbuf[:, i, :]
        for s4 in range(4):
            act(junk, x_i[:, s4 * 512:(s4 + 1) * 512], AF.Square, reads=[res("x", i)], writes=[res("junk"), res("ssA", i)],
                accum=ssA[:, i * 4 + s4:i * 4 + s4 + 1])
        dve(lambda e: e.reduce_sum(out=ssS[:, i:i + 1], in_=ssA[:, i * 4:i * 4 + 4], axis=AX.X), [res("ssA", i)], [res("ssS", i)])
        rsqrt(rsA[:, i:i + 1], ssS[:, i:i + 1], 1.0 / D, EPS, [res("ssS", i)], [res("rsA", i)])
        b = HB.nxt()
        ts(hb[:, b, :], x_i, rsA[:, i:i + 1], None, ALU.mult, None, [res("x", i), res("rsA", i)], [res("hb", b)])
        return b

    def phase_a_pe(l, i, b, extra=()):
        for half in range(2):
            q = PT.nxt()
            transposes(q, [hb[:, b, (half * 8 + k) * 128:(half * 8 + k + 1) * 128] for k in range(8)], [res("hb", b)])
            gc = mk(gcol, l * 16 + half * 8, [[1, 8], [0, 128]])
            S.add("dve", (lambda e, half=half, q=q, gc=gc: e.tensor_tensor(
                out=hT[:, half * 8:half * 8 + 8, i * 128:(i + 1) * 128], in0=ptb[q][:, :].rearrange("p (a b) -> p a b", a=8), in1=gc, op=ALU.mult)),
                reads=[res("pt", q), res("gcol")], writes=[res("hT", i)], extra=extra)

    def phase_a(l, i):
        b = phase_a_pre(l, i)
        phase_a_pe(l, i, b)

    def proj(actT, actres, i, slot, nk=16):
        u = PJ.nxt()
        pairs = [(actT[:, kc, i * 128:(i + 1) * 128], wbuf[slot][:, kc, :]) for kc in range(nk)]
        mm_group(psh(u), pairs, [res(actres, i), res("wbuf", slot)], [BK(u)])
        return u

    def rope(u, i, l, h, is_k, dst):
        kind = 1 if i == 8 else 0
        col = (8 if is_k else 0) + h * 2 + kind
        sc = dqk[:, col:col + 1]
        b = RB.nxt()
        src = psh(u).rearrange("p (a b) -> p a b", a=2)
        cosb = mk(csr, (i % 2) * 256, [[0, 2], [1, 128]])
        sinb = mk(csr, (i % 2) * 256 + 128, [[0, 2], [1, 128]])
        t13 = T13[:, b, :].rearrange("p (a b) -> p a b", a=2)
        t42 = T42[:, b, :].rearrange("p (a b) -> p a b", a=2)
        stt(t13, src, sc, cosb, ALU.mult, ALU.mult, [BK(u), res("cs", i % 2), res("dqk")], [res("T13", b)])
        stt(t42, src, sc, sinb, ALU.mult, ALU.mult, [BK(u), res("cs", i % 2), res("dqk")], [res("T42", b)])
        return b

    def rope_fin(b, dst, dres):
        eng_ = "pool" if DBG.get("pool_rope", False) else "dve"
        S.add(eng_, lambda e: e.tensor_tensor(out=dst[:, 0:128], in0=T13[:, b, 0:128], in1=T42[:, b, 128:256], op=ALU.subtract),
              reads=[res("T13", b), res("T42", b)], writes=dres)
        S.add(eng_, lambda e: e.tensor_tensor(out=dst[:, 128:256], in0=T13[:, b, 128:256], in1=T42[:, b, 0:128], op=ALU.add),
              reads=[res("T13", b), res("T42", b)], writes=dres)

    def load_cs(i):
        dma("sp", csr[:, i % 2, :], cs_d[i, :, :], writes=[res("cs", i % 2)])

    def proj_split(actT, actres, i, slot):
        u = PJ.nxt()
        pairs = [(actT[:, kc, i * 128:(i + 1) * 128], wbuf[slot][:, kc, :]) for kc in range(16)]

        def pa():
            mm_group(psh(u), pairs[:8], [res(actres, i), res("wbuf", slot)], [BK(u)], first=True, last=False)

        def pb_():
            mm_group(psh(u), pairs[8:], [res(actres, i), res("wbuf", slot)], [BK(u)], first=False, last=True)
        return u, pa, pb_

    def k_pass(l, h, scan_h=None):
        slot = load_block(w_in_l[l], 3072 + 256 * h)
        prev = None
        for i in range(NT + 1):
            if i < NT:
                load_cs(i)
                u, pa, pb_ = proj_split(hT, "hT", i, slot)
                if scan_h is not None:
                    if i >= 1:
                        scan_part3a(l, scan_h, i - 1)
                    scan_part1(l, scan_h, i)
                pa()
                if scan_h is not None:
                    if i >= 1:
                        scan_part3b(l, scan_h, i - 1)
                    scan_part2(l, scan_h, i)
                pb_()
                b = rope(u, i, l, h, True, None)
                rope_fin(b, ktok[:, i, :], [res("ktok", i)])
            if prev is not None:
                j = prev
                q = PT.nxt()
                transposes(q, [ktok[:, j, 0:128], ktok[:, j, 128:256]], [res("ktok", j)])
                act(kT[:, :, j * 128:(j + 1) * 128], ptb[q][:, 0:256].rearrange("p (a b) -> p a b", a=2), AF.Copy,
                    [res("pt", q)], [res("kT", j)])
            prev = i if i < NT else None

    def v_pass(l, h, sample_h=None):
        slot = load_block(w_in_l[l], 4096 + 256 * h)
        spread = sample_h is not None
        for i in range(NT):
            u = proj(hT, "hT", i, slot)
            if spread and i == NT - 1:
                sample_end(l, sample_h)
            act(vall[:, i, :], psh(u), AF.Copy, [BK(u)], [res("vall", i)])
            if spread and i < 8:
                sample_batch(l, sample_h, 2 * i)
                sample_batch(l, sample_h, 2 * i + 1)
        if spread:
            scan_part3(l, sample_h, NT - 1)

    def u_exchange(l, h):
        g = GAM[h]
        UH = (10, 6)
        pairs0, pairs1 = [], []
        for c in range(8):
            b = c % 4
            kbuf = kUr[:, b, :] if b < 2 else kdr[:, b - 2, :]
            kres = res("kUr", b) if b < 2 else res("kdr", b - 2)
            act(kbuf, ktok[:, c, :], AF.Copy, [res("ktok", c)], [kres], scale=float(g ** (128 * (8 - c))))
            for kc2 in range(2):
                def fn(eng, c=c, kc2=kc2, kbuf=kbuf):
                    return eng.matmul(psh(UH[kc2]), kbuf[:, kc2 * 128:(kc2 + 1) * 128], vall[:, c, :], start=(c == 0), stop=(c == 7))
                S.add("pe", fn, reads=[kres, res("vall", c)], writes=[BK(UH[kc2])])
        for kc2 in range(2):
            act(Uev[:, kc2, :], psh(UH[kc2]), AF.Copy, [BK(UH[kc2])], [res("Uev", kc2)])
        dma("sp", bS_in[l][h].ap().rearrange("(kc p) e -> p kc e", p=128), Uev[:, :, :], reads=[res("Uev", 0), res("Uev", 1)], writes=[res("bSi", l, h)])

        def cc(eng):
            return eng.collective_compute("AllGather", ALU.bypass, replica_groups=PAIRS,
                                          ins=[bS_in[l][h].ap().opt()], outs=[bS_out[l][h].ap().opt()])
        S.add("pool", cc, reads=[res("bSi", l, h)], writes=[res("bSo", l, h)], kind="cc")

    def s_init(l, h):
        dma("sp", Sf[:, :, :], bS_out[l][h].ap()[0:256, :].rearrange("(kc p) e -> p kc e", p=128), reads=[res("bSo", l, h)], writes=[res("Sf")])
        ts(Sf[:, :, :], Sf[:, :, :], flag[:, 0:1], None, ALU.mult, None, [res("Sf"), res("flag")], [res("Sf")])
        act(Sbf[:, 0, :, :], Sf[:, :, :], AF.Copy, [res("Sf")], [res("Sbf", 0)])

    def gr_pass(l, h, sample_h=None):
        slot = load_block(w_in_l[l], 5120 + 256 * h)
        spread = sample_h is not None
        for i in range(NT):
            u = proj(hT, "hT", i, slot)
            if spread and i == NT - 1:
                sample_end(l, sample_h)
                scan_part3(l, sample_h, NT - 1)
            act(sgr[:, i, :], psh(u), AF.Silu, [BK(u)], [res("sgr", i)])
            if spread and i < 7:
                sample_batch(l, sample_h, 9 + i)

    QTU = {"ids": []}

    def qT_(c, kc):
        return qTall[:, kc, c * 128:(c + 1) * 128]

    def scan_part1(l, h, c):
        g = GAM[h]
        QTU["ids"].append(mm_group(psh(6)[:, 0:128], [(kT[:, kc, c * 128:(c + 1) * 128], qT_(c, kc)) for kc in range(2)],
                                   [res("kT", c), res("qTall", c)], [BK(6)]))
        sm = c % 2
        tt(scm[:, sm, :], psh(6)[:, 0:128], masks[:, 1 if c == 8 else 0, :], ALU.mult, [BK(6), res("masks")], [res("scm", sm)])
        if c < 8:
            kb = c % 2
            act(kdr[:, kb, :], ktok[:, c, :], AF.Copy, [res("ktok", c)], [res("kdr", kb)], scale=float(g ** 128))
        else:
            qm_dst = mk(qm, 0, [[16 * 128, 2], [136, 16], [1, 8]])
            q_src = mk(qTall, 8 * 128, [[TOK, 2], [8, 16], [1, 8]])
            QTU["ids"].append(dve(lambda e: e.tensor_copy(out=qm_dst, in_=q_src), [res("qTall", 8)], [res("qm")]))
            k_b = mk(ktok, 8 * 256, [[0, 16], [1, 256]])
            m_b = mk(bm8, h * 16, [[1, 16], [0, 256]])
            tt(km[:, :, :], k_b, m_b, ALU.mult, [res("ktok", 8), res("bm8")], [res("km")])

    def scan_part2(l, h, c):
        g = GAM[h]
        sm = c % 2
        if c < 8:
            sb_cur, sb_nxt = c % 2, (c + 1) % 2
            kb = c % 2
            pairs = [(scm[:, sm, :], vall[:, c, :])] + [(qT_(c, kc), Sbf[:, sb_cur, kc, :]) for kc in range(2)]
            QTU["ids"].append(mm_group(psh(8), pairs, [res("scm", sm), res("vall", c), res("qTall", c), res("Sbf", sb_cur)], [BK(8)]))
            for kc2 in range(2):
                mm_group(psh(10 + kc2), [(kdr[:, kb, kc2 * 128:(kc2 + 1) * 128], vall[:, c, :])], [res("kdr", kb), res("vall", c)], [BK(10 + kc2)])
            stt(Sf[:, :, :], Sf[:, :, :], float(g ** 128), psb[5][:, :].rearrange("p (a b) -> p a b", a=2), ALU.mult, ALU.add,
                [res("Sf"), BK(10), BK(11)], [res("Sf")])
            act(Sbf[:, sb_nxt, :, :], Sf[:, :, :], AF.Copy, [res("Sf")], [res("Sbf", sb_nxt)])
            if c == 7:
                dma("sp", nrp[l, h].rearrange("(kc p) e -> p kc e", p=128), Sf[:, :, :], reads=[res("Sf")])
        else:
            sample_begin(l, h)
            if SAMPLE_INLINE["on"]:
                for b in range(16):
                    sample_batch(l, h, b)
                sample_end(l, h)
            return
        y_finish(c)

    SAMPLE_INLINE = {"on": False}

    def y_finish(c):
        yb = c % 2
        act(junk[:, 0:256], psh(8), AF.Square, [BK(8)], [res("junk"), res("ssy", yb)], accum=ssy[:, yb:yb + 1])
        act(Uev[:, yb, :], psh(8), AF.Copy, [BK(8)], [res("Uev", yb)])
        act(rsy[:, yb:yb + 1], ssy[:, yb:yb + 1], AF.Sqrt, [res("ssy", yb)], [res("rsy", yb)], scale=1.0 / 256.0, bias=float(EPS))

    def sample_begin(l, h):
        c = 8
        sm = c % 2
        sview = sret[l].rearrange("b h (kc p) e -> b h p kc e", p=128)
        act(v8s[:, :], vall[:, c, :], AF.Copy, [res("vall", c)], [res("v8s")])
        S.add("pe", lambda eng: eng.matmul(psh(8), scm[:, sm, :], v8s[:, :], start=True, stop=False),
              reads=[res("scm", sm), res("v8s")], writes=[BK(8)])
        for b in range(3):
            dma("sp", Sin[:, b % 3, :, :], sview[b, h], writes=[res("Sin", b % 3)])

    def sample_batch(l, h, b):
        sb_ = b % 3
        act(Sbb[:, sb_, :, :], Sin[:, sb_, :, :], AF.Copy, [res("Sin", sb_)], [res("Sbb", sb_)])
        if b >= 1:
            sample_stage_b(l, h, b - 1)

    def sample_stage_b(l, h, b):
        g = GAM[h]
        NB = 16
        sview = sret[l].rearrange("b h (kc p) e -> b h p kc e", p=128)
        oview = nrs[l].rearrange("b h (kc p) e -> b h p kc e", p=128)
        sb_ = b % 3
        so_ = b % 2

        def yf(eng, b=b, sb_=sb_):
            ins = None
            for kc in range(2):
                ins = eng.matmul(psh(8), qm[:, kc, b, :], Sbb[:, sb_, kc, :], start=False, stop=(b == NB - 1 and kc == 1))
            return ins
        S.add("pe", yf, reads=[res("qm"), res("Sbb", sb_), BK(8)], writes=[BK(8)])
        bank = 5 if b % 2 == 0 else 3
        for kc2 in range(2):
            mm_group(psh(2 * bank + kc2), [(km[:, b, kc2 * 128:(kc2 + 1) * 128], v8s[:, :])], [res("km"), res("v8s")],
                     [BK(2 * bank + kc2)])
        stt(Sout[:, so_, :, :], Sin[:, sb_, :, :], float(g ** 8), psb[bank][:, :].rearrange("p (a b) -> p a b", a=2), ALU.mult, ALU.add,
            [res("Sin", sb_), BK(2 * bank), BK(2 * bank + 1)], [res("Sout", so_)])
        dma("sp", oview[b, h], Sout[:, so_, :, :], reads=[res("Sout", so_)])
        if b + 3 < NB:
            dma("sp", Sin[:, sb_, :, :], sview[b + 3, h], writes=[res("Sin", sb_)])

    def sample_end(l, h):
        sample_stage_b(l, h, 15)
        y_finish(8)

    def scan_part3(l, h, c):
        scan_part3a(l, h, c)
        scan_part3b(l, h, c)

    def scan_part3a(l, h, c):
        yb = c % 2
        dve(lambda e: e.reciprocal(out=rsy[:, yb:yb + 1], in_=rsy[:, yb:yb + 1]), [res("rsy", yb)], [res("rsy", yb)])
        stt(rety[:, yb, :], Uev[:, yb, :], rsy[:, yb:yb + 1], sgr[:, c, :], ALU.mult, ALU.mult,
            [res("Uev", yb), res("rsy", yb), res("sgr", c)], [res("rety", yb)])

    def scan_part3b(l, h, c):
        yb = c % 2
        q = PT.nxt()
        transposes(q, [rety[:, yb, 0:128], rety[:, yb, 128:256]], [res("rety", yb)])
        act(mixT[:, 8 + 2 * h:10 + 2 * h, c * 128:(c + 1) * 128], ptb[q][:, 0:256].rearrange("p (a b) -> p a b", a=2), AF.Copy,
            [res("pt", q)], [res("mixT", c)])

    def scan_alone(l, h):
        inline = not DBG.get("sample_in_c", True)
        SAMPLE_INLINE["on"] = inline
        for c in range(NT):
            if c >= 1:
                scan_part3a(l, h, c - 1)
            scan_part1(l, h, c)
            if c >= 1:
                scan_part3b(l, h, c - 1)
            scan_part2(l, h, c)
        if inline:
            scan_part3(l, h, NT - 1)
        SAMPLE_INLINE["on"] = False

    def q_pass(l, h, uall_users):
        QTU["ids"] = []
        s_init(l, h)
        slot = load_block(w_in_l[l], 2048 + 256 * h)
        prev = None
        for i in range(NT + 1):
            if i < NT:
                load_cs(i)
                u = proj(hT, "hT", i, slot)
                b = rope(u, i, l, h, False, None)
                qb = i % 2
                rope_fin(b, qtok[:, qb, :], [res("qtok", qb)])
            if prev is not None:
                j = prev
                qb2 = j % 2
                q = PT.nxt()
                transposes(q, [qtok[:, qb2, 0:128], qtok[:, qb2, 128:256]], [res("qtok", qb2)])
                act(qTall[:, :, j * 128:(j + 1) * 128], ptb[q][:, 0:256].rearrange("p (a b) -> p a b", a=2), AF.Copy, [res("pt", q)],
                    [res("qTall", j)], extra=uall_users)
            prev = i if i < NT else None

    def pool_prep(l):
        dma("pool", wpool[:, :, :], w_pool[l].rearrange("g (kc p) d -> p (g kc) d", p=128), writes=[res("wpool")])
        dma("sp", pst, bcast_row(pool_scale, l * 1024, 1024), writes=[res("pst")])
        w4 = mk(wpool, 0, [[512, 4], [256, 2], [1, 256]])
        p4 = mk(pst, 0, [[256, 4], [0, 2], [1, 256]])
        tt(w4, w4, p4, ALU.mult, [res("wpool"), res("pst")], [res("wpool")])
        dma("sp", nps[l][:, 0:7, :], spool[l][:, 8:15, :])

    UALLU = {"ids": []}

    def u_pass(l, g):
        UALLU["ids"] = []
        slot = load_block(w_in_l[l], 256 * g)
        for hf in range(2):
            src = spool[l][8 * hf:8 * hf + 8].rearrange("b r c -> (b r) c")[:, 256 * g:256 * g + 256]
            dma("pool", mk(bufS, hf * 256, [[1, 256]], parts=120), src, writes=[res("bufS", hf)])
        for i in [7, 0, 1, 2, 3, 4, 5, 6, 8]:
            u = proj(hT, "hT", i, slot)
            act(uall[:, i, :], psh(u), AF.Copy, [BK(u)], [res("uall", i)], extra=list(QTU["ids"]))
            if i >= 7:
                act(uf32[:, i - 7, :], psh(u), AF.Copy, [BK(u)], [res("uf32", i - 7)])
            if i == 7:
                dma("sp", npp[l][:, 256 * g:256 * g + 256], uf32[113:128, 0, :], reads=[res("uf32", 0)])
                UALLU["ids"].append(dma("sp", bU_in[l][g].ap(), uall[:, 7, :], reads=[res("uall", 7)], writes=[res("bUi", l, g)]))

                def cc(eng):
                    return eng.collective_compute("AllGather", ALU.bypass, replica_groups=PAIRS,
                                                  ins=[bU_in[l][g].ap().opt()], outs=[bU_out[l][g].ap().opt()])
                S.add("pool", cc, reads=[res("bUi", l, g)], writes=[res("bUo", l, g)], kind="cc")
            if i == 8:
                for b in range(16):
                    dma("sp", nps[l][b, 7:15, 256 * g:256 * g + 256], uf32[8 * b:8 * b + 8, 1, :], reads=[res("uf32", 1)])

    def gp_pass_pool(l, g):
        slot = load_block(w_in_l[l], 1024 + 256 * g)
        order = [1, 2, 3, 4, 5, 6, 7, 8, 0]
        prev = None
        for n in range(NT + 1):
            if n < NT:
                i = order[n]
                u = proj(hT, "hT", i, slot)
                sb_ = n % 2
                act(sgpr[:, sb_, :], psh(u), AF.Silu, [BK(u)], [res("sgpr", sb_)])
            if prev is not None:
                if order[prev] == 0:
                    dma("sp", uprev_raw, bU_out[l][g].ap()[0:128, :], reads=[res("bUo", l, g)], writes=[res("uprev_raw")])
                    ts(uprevX, uprev_raw, flag[:, 0:1], None, ALU.mult, None, [res("uprev_raw"), res("flag")], [res("uprevX")])
                pool_tile(l, g, order[prev], prev % 2)
            prev = n if n < NT else None
        pool_tile_t(l, g, *PTL["prev"])
        PTL["prev"] = None

    def pool_tile(l, g, i, sgb):
        pbuf = sgb
        if i == 0:
            kind = 2
        elif i == 8:
            kind = 3
        else:
            kind = 0
        for half in range(2):
            out = psh(6)[:, half * 128:(half + 1) * 128]
            pairs = [(uall[:, i, half * 128:(half + 1) * 128], amat[:, g * 4 + kind, :])]
            rd = [res("uall", i), res("amat")]
            if i == 0:
                pairs.append((uprevX[:, half * 128:(half + 1) * 128], amat[:, g * 4 + 1, :]))
                rd.append(res("uprevX"))
            elif i < 8:
                pairs.append((uall[:, i - 1, half * 128:(half + 1) * 128], amat[:, g * 4 + 1, :]))
                rd.append(res("uall", i - 1))
            if i < 8:
                UALLU["ids"].append(mm_group(out, pairs, rd, [BK(6)]))
            else:
                def fn(eng, half=half, out=out, pairs=pairs):
                    ins = None
                    for hf in range(2):
                        eng.matmul(out[:, hf * 64:(hf + 1) * 64], pairs[0][0], pairs[0][1][:, hf * 64:(hf + 1) * 64], start=True, stop=False)
                        ins = eng.matmul(out[:, hf * 64:(hf + 1) * 64], mk(bufS, hf * 256 + half * 128, [[1, 128]], parts=120),
                                         mk(assm, g * 64, [[1, 64]], parts=120), start=False, stop=True)
                    return ins
                UALLU["ids"].append(S.add("pe", fn, reads=rd + [res("bufS", 0), res("bufS", 1), res("ass")], writes=[BK(6)]))
        act(pooledT[:, pbuf, :, :], psh(6).rearrange("p (a b) -> p a b", a=2), AF.Copy, [BK(6)], [res("pooledT", pbuf)])
        if PTL["prev"] is not None:
            pool_tile_t(l, g, *PTL["prev"])
        mm_group(psh(8), [(pooledT[:, pbuf, half, :], wpool[:, g * 2 + half, :]) for half in range(2)],
                 [res("pooledT", pbuf), res("wpool")], [BK(8)])
        tt(pooly[:, pbuf, :], psh(8), sgpr[:, sgb, :], ALU.mult, [BK(8), res("sgpr", sgb)], [res("pooly", pbuf)])
        PTL["prev"] = (i, pbuf)

    PTL = {"prev": None}

    def pool_tile_t(l, g, i, pbuf):
        q = PT.nxt()
        transposes(q, [pooly[:, pbuf, 0:128], pooly[:, pbuf, 128:256]], [res("pooly", pbuf)])
        act(mixT[:, 2 * g:2 * g + 2, i * 128:(i + 1) * 128], ptb[q][:, 0:256].rearrange("p (a b) -> p a b", a=2), AF.Copy,
            [res("pt", q)], [res("mixT", i)])

    def phase_b(l):
        dve(lambda e: e.memset(qm[:, :, :, :], 0.0), [], [res("qm")])
        pool_prep(l)
        QTU["ids"] = []
        cut = DBG.get("cut", 10 ** 9)
        n = [0]

        def tick():
            n[0] += 1
            if n[0] >= cut:
                raise StopIteration
        for s_ in range(4):
            k_pass(l, s_, scan_h=(s_ - 1 if s_ >= 1 else None)); tick()
            v_pass(l, s_, sample_h=(s_ - 1 if s_ >= 1 else None)); tick()
            u_exchange(l, s_); tick()
            gr_pass(l, s_); tick()
            u_pass(l, s_); tick()
            gp_pass_pool(l, s_); tick()
            q_pass(l, s_, list(UALLU["ids"])); tick()
        scan_alone(l, 3)

    ZREAD = {"ids": []}

    def phase_c(l):
        xsrc = xin if l == 0 else xpark
        ride = DBG.get("sample_in_c", True)

        FX = {"ids": (), "n": 0}

        def load_x1(i, extra=()):
            S.add("sp", (lambda eng, i=i: eng.dma_start(out=xbuf[:, i, :], in_=xsrc[i * 128:(i + 1) * 128, :])),
                  writes=[res("x", i)], kind="d", extra=extra)

        def load_x(extra=()):
            for i in range(NT):
                load_x1(i, extra)
        if not ride:
            load_x()
        dma("pool", pbf[:, :, :], pin[l].rearrange("(i p) c -> p i c", p=128), writes=[res("pbf")])
        dma("sp", gtab[:, 0, :], bcast_row(post_g, l * D, 256), writes=[res("gtab", 0)])
        for j in range(8):
            slot = load_block(w_out_l[l], 256 * j)
            gb = j % 2
            if j + 1 < 8:
                dma("sp", gtab[:, (j + 1) % 2, :], bcast_row(post_g, l * D + 256 * (j + 1), 256), writes=[res("gtab", (j + 1) % 2)])
            for i in range(NT):
                if ride and j == 0 and i == NT - 1:
                    sample_end(l, 3)
                    scan_part3(l, 3, NT - 1)
                u = proj(mixT, "mixT", i, slot)
                sq = act(junk[:, 0:256], psh(u), AF.Square, [BK(u)], [res("junk"), res("ssz", i)], accum=ssz[:, i * 8 + j:i * 8 + j + 1])
                S.add("dve", (lambda e, i=i, j=j, u=u, gb=gb: e.tensor_tensor(out=zb[:, i, j * 256:(j + 1) * 256], in0=psh(u), in1=gtab[:, gb, :], op=ALU.mult)),
                      reads=[BK(u), res("gtab", gb)], writes=[res("z", i)], extra=[sq])
                if ride and j == 0 and i < 8:
                    sample_batch(l, 3, 2 * i)
                    sample_batch(l, 3, 2 * i + 1)
                if ride and 1 <= j <= 3 and i in (0, 3, 6):
                    load_x1(FX["n"], FX["ids"])
                    FX["n"] += 1
            if ride and j == 0:
                FX["ids"] = S.fence()
        if DBG.get("cd_merge", True):
            phase_d_pre(l)
        ZREAD["ids"] = []
        for i in range(NT):
            dve(lambda e, i=i: e.reduce_sum(out=rsz[:, i:i + 1], in_=ssz[:, i * 8:i * 8 + 8], axis=AX.X), [res("ssz", i)], [res("rsz", i)])
            rsqrt(rsz[:, i:i + 1], rsz[:, i:i + 1], 1.0 / D, EPS, [res("rsz", i)], [res("rsz", i)])
        for i in range(NT):
            ZREAD["ids"].append(stt(xbuf[:, i, :], zb[:, i, :], rsz[:, i:i + 1], xbuf[:, i, :], ALU.mult, ALU.add,
                                    [res("z", i), res("rsz", i), res("x", i)], [res("x", i)]))
            b = HB.nxt()
            act(hb[:, b, :], xbuf[:, i, :], AF.Copy, [res("x", i)], [res("hb", b)])
            for half in range(2):
                q = PT.nxt()
                transposes(q, [hb[:, b, (half * 8 + k) * 128:(half * 8 + k + 1) * 128] for k in range(8)], [res("hb", b)])
                act(mixT[:, half * 8:half * 8 + 8, i * 128:(i + 1) * 128], ptb[q][:, :].rearrange("p (a b) -> p a b", a=8), AF.Copy,
                    [res("pt", q)], [res("mixT", i)])

    PA = {"b": 0}

    def load_wple(l, idx):
        j = idx % 8
        wb = idx % 2
        src = w_ple[l].rearrange("(kc p) c -> p kc c", p=128)[:, :, 256 * j:256 * j + 256]
        dma("pool", wple[:, wb, :, :], src, writes=[res("wple", wb)])

    def phase_d_pre(l):
        for i in range(NT):
            q = PT.nxt()
            transposes(q, [pbf[:, i, 0:128], pbf[:, i, 128:256]], [res("pbf")])
            act(pT[:, :, i * 128:(i + 1) * 128], ptb[q][:, 0:256].rearrange("p (a b) -> p a b", a=2), AF.Copy, [res("pt", q)], [res("pT", i)])

        load_wple(l, 0)
        for j in range(8):
            load_wple(l, j + 1)
            wb = j % 2
            for i in range(NT):
                u = PJ.nxt()
                mm_group(psh(u), [(pT[:, kc, i * 128:(i + 1) * 128], wple[:, wb, kc, :]) for kc in range(2)],
                         [res("pT", i), res("wple", wb)], [BK(u)])
                act(junk[:, 0:256], psh(u), AF.Square, [BK(u)], [res("junk"), res("ssp", i)], accum=ssp[:, i * 8 + j:i * 8 + j + 1])
        for i in range(NT):
            dve(lambda e, i=i: e.reduce_sum(out=rsp[:, i:i + 1], in_=ssp[:, i * 8:i * 8 + 8], axis=AX.X), [res("ssp", i)], [res("rsp", i)])
            rsqrt(rsp[:, i:i + 1], rsp[:, i:i + 1], 1.0 / D, EPS, [res("rsp", i)], [res("rsp", i)])

    def phase_d(l):
        if not DBG.get("cd_merge", True):
            phase_d_pre(l)
        for j in range(8):
            if j < 7:
                load_wple(l, 8 + j + 1)
            slot = load_block(w_gate_l[l], 256 * j)
            wb = j % 2
            gb = (2 * j) % 4
            dma("sp", gtab[:, gb, :], bcast_row(b_gate, l * D + 256 * j, 256), writes=[res("gtab", gb)])
            dma("sp", gtab[:, gb + 1, :], bcast_row(ple_g, l * D + 256 * j, 256), writes=[res("gtab", gb + 1)])
            for i in range(NT):
                ua = proj(mixT, "mixT", i, slot)
                ub = PJ.nxt()
                mm_group(psh(ub), [(pT[:, kc, i * 128:(i + 1) * 128], wple[:, wb, kc, :]) for kc in range(2)],
                         [res("pT", i), res("wple", wb)], [BK(ub)])
                t1 = TMP.nxt()
                t2 = TMP.nxt()
                tt(tmpf[:, t1, :], psh(ua), gtab[:, gb, :], ALU.add, [BK(ua), res("gtab", gb)], [res("tmpf", t1)])
                act(tmpf[:, t1, :], tmpf[:, t1, :], AF.Sigmoid, [res("tmpf", t1)], [res("tmpf", t1)])
                stt(tmpf[:, t2, :], psh(ub), rsp[:, i:i + 1], gtab[:, gb + 1, :], ALU.mult, ALU.mult,
                    [BK(ub), res("rsp", i), res("gtab", gb + 1)], [res("tmpf", t2)])
                tt(tmpf[:, t1, :], tmpf[:, t1, :], tmpf[:, t2, :], ALU.mult, [res("tmpf", t1), res("tmpf", t2)], [res("tmpf", t1)])
                tt(xbuf[:, i, j * 256:(j + 1) * 256], xbuf[:, i, j * 256:(j + 1) * 256], tmpf[:, t1, :], ALU.add,
                   [res("x", i), res("tmpf", t1)], [res("x", i)])
                if j == 7:
                    if l == 0:
                        if i >= 1:
                            phase_a_pe(1, i - 1, PA["b"], extra=list(ZREAD["ids"]))
                        PA["b"] = phase_a_pre(1, i)
                        dma("sp", xpark[i * 128:(i + 1) * 128, :], xbuf[:, i, :], reads=[res("x", i)], writes=[res("xpark", i)])
                        if i == NT - 1:
                            phase_a_pe(1, i, PA["b"], extra=list(ZREAD["ids"]))
                    else:
                        dma("sp", yout[i * 128:(i + 1) * 128, :], xbuf[:, i, :], reads=[res("x", i)])

    w_in_l = [w_in[l] for l in range(DEPTH)]
    w_out_l = [w_out[l] for l in range(DEPTH)]
    w_gate_l = [w_gate[l] for l in range(DEPTH)]
    for l in range(DEPTH):
        for s_ in range(4):
            for c0 in (3072 + 256 * s_, 4096 + 256 * s_, 5120 + 256 * s_, 256 * s_, 1024 + 256 * s_, 2048 + 256 * s_):
                plan_block(w_in_l[l], c0)
        for j in range(8):
            plan_block(w_out_l[l], 256 * j)
        for j in range(8):
            plan_block(w_gate_l[l], 256 * j)
    issue_upto(3)
    load_consts()
    for i in range(NT):
        dma("sp", xbuf[:, i, :], xin[i * 128:(i + 1) * 128, :], writes=[res("x", i)])
    for i in range(NT):
        if stop_after != "consts":
            phase_a(0, i)
    stages = []
    for l in range(DEPTH):
        stages += [("B", l), ("C", l), ("D", l)]
    if stop_after in ("consts", "A"):
        stages = None
    for (ph, l) in (stages or []):
        if not (ph == "C" and DBG.get("sample_in_c", True)) and not (ph == "D" and DBG.get("cd_merge", True)):
            S.barrier()
        if ph == "B":
            try:
                phase_b(l)
            except StopIteration:
                break
        elif ph == "C":
            phase_c(l)
        else:
            phase_d(l)
        if stop_after == (ph, l):
            break
    S.finish()

    with nc.Block() as block:
        S.emit(nc, block, sems, dsems, ccsem)
    es.close()
    return nc


def _bf(a):
    return np.ascontiguousarray(a.astype(ml_dtypes.bfloat16))


def _tables(core):
    odd = core % 2
    start = 1024 * odd
    half = 128
    inv = (np.float32(10000.0) ** (-(np.arange(half, dtype=np.float32)) / np.float32(half))).astype(np.float32)
    cs = np.zeros((NT, 128, 256), np.float32)
    p = np.arange(128)
    for i in range(NT):
        pos = (start + 128 * i + p) if i < 8 else (16384 + (p % 8))
        ang = (pos.astype(np.float32)[:, None] * inv[None, :]).astype(np.float32)
        cs[i, :, :128] = np.cos(ang.astype(np.float64))
        cs[i, :, 128:] = np.sin(ang.astype(np.float64))
    dqk = np.zeros((128, 16), np.float64)
    bm8 = np.zeros((128, 4, 16), np.float64)
    for h in range(4):
        g = GAM[h]
        dqk[:, h * 2 + 0] = g ** (p + 1.0)
        dqk[:, h * 2 + 1] = g ** ((p % 8) + 1.0)
        dqk[:, 8 + h * 2 + 0] = g ** (-(p + 1.0)) / 16.0
        dqk[:, 8 + h * 2 + 1] = g ** (-((p % 8) + 1.0)) / 16.0
        for b in range(16):
            bm8[:, h, b] = (p // 8 == b) * (g ** 8)
    masks = np.zeros((128, 2, 128), np.float32)
    jj, ii = np.meshgrid(p, p, indexing="ij")
    masks[:, 0, :] = (ii >= jj)
    masks[:, 1, :] = (ii >= jj) & (ii // 8 == jj // 8)
    amat = np.zeros((128, 4, 4, 128), np.float64)
    ass = np.zeros((128, 4, 64), np.float64)
    ss, tt_ = np.meshgrid(p, p, indexing="ij")
    for g, w in enumerate(WINS):
        inwin = (ss <= tt_) & (ss >= tt_ - w + 1)
        amat[:, g, 0, :] = inwin / w - (ss == tt_)
        amat[:, g, 1, :] = ((ss - 128) >= (tt_ - w + 1)) / w
        if odd:
            amat[:, g, 2, :] = amat[:, g, 0, :]
        else:
            cnt = np.minimum(tt_ + 1, w)
            amat[:, g, 2, :] = inwin / cnt - (ss == tt_)
        sb_, st_ = ss // 8, ss % 8
        tb_, ttt = tt_ // 8, tt_ % 8
        inw = (st_ <= ttt) & (st_ >= ttt - w + 1) & (sb_ == tb_)
        amat[:, g, 3, :] = inw / w - (ss == tt_)
        for r in range(120):
            b, rr = r // 15, r % 15
            for t in range(8):
                if rr >= 16 + t - w:
                    ass[r, g, b * 8 + t] = 1.0 / w
    return dict(
        cs=cs, dqk=dqk.astype(np.float32), bm8=bm8.reshape(128, 64).astype(np.float32),
        masks=_bf(masks.reshape(128, 256)), amat=_bf(amat.reshape(128, 2048)), ass=_bf(ass.reshape(128, 256)),
        ident=_bf(np.eye(128, dtype=np.float32)), flag=np.full((128, 1), float(odd), np.float32),
    )


_NC_CACHE = {}


def kernel(x_prompt, x_sample, state_ret, state_pool, p_prompt, p_sample,
           w_in, w_pool, pool_scale, w_out, pre_g, post_g, w_ple, ple_g, w_ple_gate, b_ple_gate, _stop_after=None):
    f = lambda a: np.ascontiguousarray(np.asarray(a, dtype=np.float32))
    x_prompt, x_sample, state_ret, state_pool, p_prompt, p_sample = map(f, (x_prompt, x_sample, state_ret, state_pool, p_prompt, p_sample))
    w_in, w_pool, pool_scale, w_out, pre_g, post_g, w_ple, ple_g, w_ple_gate, b_ple_gate = map(
        f, (w_in, w_pool, pool_scale, w_out, pre_g, post_g, w_ple, ple_g, w_ple_gate, b_ple_gate))
    key = _stop_after
    if key not in _NC_CACHE:
        _NC_CACHE[key] = build_program(_stop_after)
    nc = _NC_CACHE[key]
    gcol = np.ascontiguousarray(pre_g.reshape(DEPTH, 16, 128).transpose(2, 0, 1).reshape(128, DEPTH * 16))
    in_maps = []
    for c in range(NCORE):
        seq, hf = c // 2, c % 2
        xin = np.concatenate([x_prompt[seq, hf * 1024:(hf + 1) * 1024], x_sample[16 * c:16 * c + 16].reshape(128, D)], axis=0)
        pin = np.concatenate([p_prompt[:, seq, hf * 1024:(hf + 1) * 1024], p_sample[:, 16 * c:16 * c + 16].reshape(DEPTH, 128, 256)], axis=1)
        m = dict(xin=np.ascontiguousarray(xin), pin=np.ascontiguousarray(pin),
                 sret=np.ascontiguousarray(state_ret[:, 16 * c:16 * c + 16]), spool=np.ascontiguousarray(state_pool[:, 16 * c:16 * c + 16]),
                 w_in=w_in, w_pool=w_pool, pool_scale=pool_scale, w_out=w_out, gcol=gcol, post_g=post_g, w_ple=w_ple, ple_g=ple_g,
                 w_gate=w_ple_gate, b_gate=b_ple_gate)
        m.update(_tables(c))
        in_maps.append(m)
    if DBG.get("return_maps"):
        return nc, in_maps
    res_ = run_bass_kernel_spmd(nc, in_maps, core_ids=list(range(NCORE)))
    R = res_.results
    y_prompt = np.zeros((4, 2048, D), np.float32)
    y_sample = np.zeros((128, 8, D), np.float32)
    nrp = np.zeros((DEPTH, 4, 4, 256, 256), np.float32)
    npp = np.zeros((DEPTH, 4, 15, 1024), np.float32)
    nrs = np.zeros((DEPTH, 128, 4, 256, 256), np.float32)
    nps = np.zeros((DEPTH, 128, 15, 1024), np.float32)
    for c in range(NCORE):
        seq, hf = c // 2, c % 2
        yo = np.asarray(R[c]["yout"])
        y_prompt[seq, hf * 1024:(hf + 1) * 1024] = yo[:1024]
        y_sample[16 * c:16 * c + 16] = yo[1024:].reshape(16, 8, D)
        if hf == 1:
            nrp[:, seq] = np.asarray(R[c]["nrp"])
            npp[:, seq] = np.asarray(R[c]["npp"])
        nrs[:, 16 * c:16 * c + 16] = np.asarray(R[c]["nrs"])
        nps[:, 16 * c:16 * c + 16] = np.asarray(R[c]["nps"])
    return (y_prompt, y_sample, nrp, npp, nrs, nps)
```

```python
import numpy as np
import ml_dtypes
import concourse.bass as bass
import concourse.mybir as mybir
from concourse.bass_utils import run_bass_kernel_spmd

F32 = mybir.dt.float32
BF16 = mybir.dt.bfloat16
ALU = mybir.AluOpType
AF = mybir.ActivationFunctionType
AX = mybir.AxisListType

D = 2048
NT = 9
TOK = NT * 128
INC = 6144
DEPTH = 2
EPS = 1e-6
GAM = [1.0 - 2.0 ** (-5.0 - h) for h in range(4)]
WINS = (2, 4, 8, 16)
NCORE = 8
DBG = {}
SAME_ENGINE_WAITS = DBG.get("sew", True)


class Op:
    __slots__ = ("eng", "fn", "deps", "kind", "sig")

    def __init__(self, eng, fn, deps, kind):
        self.eng, self.fn, self.deps, self.kind, self.sig = eng, fn, deps, kind, None


class Sched:
    KRING = {"sp": 12, "pool": 6}

    def __init__(self):
        self.ops = []
        self.lastw = {}
        self.rd_c = {}
        self.rd_d = {}
        self.dma_hist = {"sp": [], "pool": []}
        self.cc_last = None
        self.open_dmas = []
        self.pending = {}

    def add(self, eng, fn, reads=(), writes=(), kind="c", extra=(), nobar=False):
        i = len(self.ops)
        deps = set(extra)
        for r in reads:
            w = self.lastw.get(r)
            if w is not None:
                deps.add(w)
        for r in writes:
            w = self.lastw.get(r)
            if w is not None:
                deps.add(w)
            deps.update(self.rd_c.get(r, {}).values())
            deps.update(self.rd_d.get(r, ()))
        if kind == "d":
            h = self.dma_hist[eng]
            k = self.KRING[eng]
            if len(h) >= k:
                deps.add(h[-k])
            h.append(i)
            if not nobar:
                self.open_dmas.append(i)
        if kind == "cc":
            if self.cc_last is not None:
                deps.add(self.cc_last)
            self.cc_last = i
            self.open_dmas.append(i)
        if eng in self.pending and not nobar:
            deps.update(self.pending.pop(eng))
        self.ops.append(Op(eng, fn, deps, kind))
        for r in reads:
            if kind in ("d", "cc"):
                self.rd_d.setdefault(r, []).append(i)
            else:
                self.rd_c.setdefault(r, {})[eng] = i
        for r in writes:
            self.lastw[r] = i
            self.rd_c[r] = {}
            self.rd_d[r] = []
        return i

    def fence(self):
        ids = set(self.open_dmas)
        seen = set()
        for i in range(len(self.ops) - 1, -1, -1):
            e = self.ops[i].eng
            if e in ("pe", "act", "dve") and e not in seen and self.ops[i].kind in ("c", "bar"):
                seen.add(e)
                ids.add(i)
            if len(seen) == 3:
                break
        return ids

    def barrier(self):
        ids = []
        for e in ("pe", "act", "dve"):
            ids.append(self.add(e, lambda eng: eng.drain(), kind="bar"))
        allids = set(ids) | set(self.open_dmas)
        self.open_dmas = []
        for e in ("pe", "act", "dve", "pool", "sp"):
            self.pending[e] = set(allids)
        keep = lambda dct: {k: v for k, v in dct.items() if k[0] == "wbuf"}
        self.lastw, self.rd_c, self.rd_d = keep(self.lastw), keep(self.rd_c), keep(self.rd_d)

    def finish(self):
        self.barrier()
        for e in ("pe", "act", "dve"):
            self.add(e, lambda eng: eng.drain(), kind="bar")
        self.add("sp", None, kind="w")
        self.add("pool", None, kind="w")

    def emit(self, nc, block, sems, dsems, ccsem):
        ops = self.ops
        used = [False] * len(ops)
        for op in ops:
            for d in op.deps:
                used[d] = True
        cnt = {e: 0 for e in ("pe", "act", "dve", "pool")}
        didx = {"sp": 0, "pool": 0}
        ccn = 0
        for i, op in enumerate(ops):
            if op.kind in ("c", "bar"):
                if used[i] or op.kind == "bar":
                    cnt[op.eng] += 1
                    op.sig = (sems[op.eng], cnt[op.eng], 1, op.eng)
            elif op.kind == "d":
                j = didx[op.eng]
                didx[op.eng] += 1
                k = self.KRING[op.eng]
                op.sig = (dsems[op.eng][j % k], 16 * (j // k + 1), 16, None)
            elif op.kind == "cc":
                ccn += 1
                op.sig = (ccsem, ccn, 1, None)
        per = {e: [] for e in ("pe", "act", "dve", "pool", "sp")}
        for op in ops:
            per[op.eng].append(op)

        def run(engname, eng):
            waited = {}
            for op in per[engname]:
                for d in sorted(op.deps):
                    dop = ops[d]
                    if dop.sig is None:
                        continue
                    sem, val, _, src = dop.sig
                    if src == engname and (engname == "pe" or not DBG.get("sew", True)):
                        continue
                    key = id(sem)
                    if waited.get(key, 0) >= val:
                        continue
                    eng.wait_ge(sem, val)
                    waited[key] = val
                if op.fn is None:
                    continue
                ins = op.fn(eng)
                if op.sig is not None:
                    if op.kind == "cc":
                        ins.then_inc(op.sig[0])
                    else:
                        ins.then_inc(op.sig[0], op.sig[2])

        @block.tensor
        def _(e):
            run("pe", e)

        @block.scalar
        def _(e):
            run("act", e)

        @block.vector
        def _(e):
            run("dve", e)

        @block.gpsimd
        def _(e):
            run("pool", e)

        @block.sync
        def _(e):
            run("sp", e)


def build_program(stop_after=None):
    nc = bass.Bass("TRN2", target_bir_lowering=False)

    def din(name, shape, dt=F32):
        return nc.dram_tensor(name, list(shape), dt, kind="ExternalInput")

    def dout(name, shape, dt=F32):
        return nc.dram_tensor(name, list(shape), dt, kind="ExternalOutput")

    xin = din("xin", [TOK, D])
    pin = din("pin", [DEPTH, TOK, 256])
    sret = din("sret", [DEPTH, 16, 4, 256, 256])
    spool = din("spool", [DEPTH, 16, 15, 1024])
    w_in = din("w_in", [DEPTH, D, INC])
    w_pool = din("w_pool", [DEPTH, 4, 256, 256])
    pool_scale = din("pool_scale", [DEPTH, 1024])
    w_out = din("w_out", [DEPTH, D, D])
    gcol_d = din("gcol", [128, DEPTH * 16])
    post_g = din("post_g", [DEPTH, D])
    w_ple = din("w_ple", [DEPTH, 256, D])
    ple_g = din("ple_g", [DEPTH, D])
    w_gate = din("w_gate", [DEPTH, D, D])
    b_gate = din("b_gate", [DEPTH, D])
    cs_d = din("cs", [NT, 128, 256])
    dqk_d = din("dqk", [128, 16])
    masks_d = din("masks", [128, 2 * 128], BF16)
    amat_d = din("amat", [128, 16 * 128], BF16)
    ass_d = din("ass", [128, 4 * 64], BF16)
    ident_d = din("ident", [128, 128], BF16)
    bm8_d = din("bm8", [128, 64])
    flag_d = din("flag", [128, 1])

    yout = dout("yout", [TOK, D])
    nrp = dout("nrp", [DEPTH, 4, 256, 256])
    npp = dout("npp", [DEPTH, 15, 1024])
    nrs = dout("nrs", [DEPTH, 16, 4, 256, 256])
    nps = dout("nps", [DEPTH, 16, 15, 1024])

    xpark = nc.dram_tensor("xpark", [TOK, D], F32)
    bS_in = [[nc.dram_tensor(f"bSi{l}{h}", [256, 256], F32) for h in range(4)] for l in range(DEPTH)]
    bS_out = [[nc.dram_tensor(f"bSo{l}{h}", [512, 256], F32) for h in range(4)] for l in range(DEPTH)]
    bU_in = [[nc.dram_tensor(f"bUi{l}{g}", [128, 256], BF16) for g in range(4)] for l in range(DEPTH)]
    bU_out = [[nc.dram_tensor(f"bUo{l}{g}", [256, 256], BF16) for g in range(4)] for l in range(DEPTH)]
    PAIRS = DBG.get("pairs", [[0, 1], [2, 3], [4, 5], [6, 7]])

    S = Sched()
    ARENA_W = 53200
    from contextlib import ExitStack
    es = ExitStack()
    arena = es.enter_context(nc.sbuf_tensor("arena", [128, ARENA_W], F32))
    psb = [es.enter_context(nc.psum_tensor(f"psb{k}", [128, 512], F32)) for k in range(6)]
    ptb = [es.enter_context(nc.psum_tensor(f"ptb{k}", [128, 1024], BF16)) for k in range(2)]
    sem_names = ("pe", "act", "dve", "pool")
    sems = {e: es.enter_context(nc.semaphore(f"s_{e}")) for e in sem_names}
    dsems = {q: [es.enter_context(nc.semaphore(f"d_{q}{k}")) for k in range(Sched.KRING[q])] for q in ("sp", "pool")}
    ccsem = es.enter_context(nc.semaphore("ccsem"))

    def view(off, dt, *shape):
        n = int(np.prod(shape))
        nb = n * (4 if dt == F32 else 2)
        assert off % 4 == 0 and nb % 4 == 0 and off + nb <= ARENA_W * 4, (off, nb)
        a = arena[:, off // 4:(off + nb) // 4]
        if dt != F32:
            a = a.bitcast(dt)
        if len(shape) == 2:
            a = a.rearrange("p (a b) -> p a b", a=shape[0])
        elif len(shape) == 3:
            a = a.rearrange("p (a b c) -> p a b c", a=shape[0], b=shape[1])
        return a

    def mk(base, off, dims, parts=128):
        pst = base.ap[0][0]
        return bass.AP(base.tensor, base.offset + off, [[pst, parts]] + [list(d) for d in dims])

    R1, R2, R3, R4, R5 = 0, 36864, 73728, 98304, 172032
    hT = view(R1, BF16, 16, TOK)
    zb = view(R1, BF16, NT, D)
    mixT = view(R2, BF16, 16, TOK)
    wbuf = [view(R3 + 8192 * k, BF16, 16, 256) for k in range(3)]
    xbuf = view(R4, F32, NT, D)

    class Bump:
        def __init__(self, off, lim):
            self.off, self.lim = off, lim

        def __call__(self, dt, *shape):
            n = int(np.prod(shape)) * (4 if dt == F32 else 2)
            n = (n + 31) // 32 * 32
            v = view(self.off, dt, *shape)
            self.off += n
            assert self.off <= self.lim, (self.off, self.lim)
            return v

    p5 = Bump(R5, ARENA_W * 4)
    ident = p5(BF16, 128)
    masks = p5(BF16, 2, 128)
    amat = p5(BF16, 16, 128)
    assm = p5(BF16, 4, 64)
    dqk = p5(F32, 16)
    bm8 = p5(F32, 4, 16)
    flag = p5(F32, 8)
    gcol = p5(F32, DEPTH * 16)
    stats = p5(F32, 256)
    junk = p5(BF16, 512)
    gtab = p5(F32, 4, 256)
    wple = p5(BF16, 2, 2, 256)
    pbf = p5(BF16, NT, 256)
    pT = p5(BF16, 2, TOK)
    tmpf = p5(F32, 4, 256)
    csr = p5(F32, 2, 256)
    v8s = p5(BF16, 256)
    sgpr = p5(BF16, 2, 256)
    hb_off = p5.off
    hb = p5(BF16, 2, D)
    ssA = stats[:, 0:36]
    ssS = stats[:, 36:45]
    rsA = stats[:, 45:54]
    ssz = stats[:, 54:126]
    rsz = stats[:, 126:135]
    ssp = stats[:, 135:207]
    rsp = stats[:, 207:216]
    ssy = stats[:, 216:224]
    rsy = stats[:, 224:232]

    pb = Bump(R4, R4 + 73728)
    kT = pb(BF16, 2, TOK)
    ktok = pb(BF16, NT, 256)
    vall = pb(BF16, NT, 256)
    sgr = pb(BF16, NT, 256)
    uall = pb(BF16, NT, 256)
    qTall = uall.rearrange("p a b -> p (a b)").rearrange("p (a b) -> p a b", a=2)
    qm = pb(BF16, 2, 16, 128)
    km = pb(BF16, 16, 256)
    Sf = pb(F32, 2, 256)
    Sbf = pb(BF16, 2, 2, 256)
    Sin = pb(F32, 3, 2, 256)
    Sout = pb(F32, 2, 2, 256)
    Sbb = pb(BF16, 3, 2, 256)
    Uev = pb(F32, 2, 256)
    uf32 = pb(F32, 2, 256)
    uprev_raw = pb(BF16, 256)
    uprevX = pb(BF16, 256)
    bufS = pb(BF16, 2, 256)
    pooledT = pb(BF16, 2, 2, 128)
    T13 = pb(F32, 2, 256)
    T42 = pb(F32, 2, 256)
    qtok = pb(BF16, 2, 256)
    scm = pb(BF16, 2, 128)
    rety = pb(BF16, 2, 256)
    pooly = pb(BF16, 2, 256)
    kdr = pb(BF16, 2, 256)
    kUr = pb(BF16, 2, 256)
    pb2 = Bump(hb_off, hb_off + 8192)
    wpool = pb2(BF16, 8, 256)
    pst = pb2(F32, 1024)

    def psh(k):
        return psb[k // 2][:, (k % 2) * 256:(k % 2) * 256 + 256]

    class RR:
        def __init__(self, items):
            self.items, self.i = items, 0

        def nxt(self):
            v = self.items[self.i % len(self.items)]
            self.i += 1
            return v

    PJ = RR([0, 2, 4])
    PT = RR([0, 1])
    TMP = RR([0, 1, 2, 3])
    HB = RR([0, 1])
    RB = RR([0, 1])

    def res(*a):
        return a

    def BK(k):
        return ("bank", k // 2)

    def dma(q, out, in_, reads=(), writes=(), nobar=False):
        def fn(eng, out=out, in_=in_):
            return eng.dma_start(out=out, in_=in_)
        return S.add(q, fn, reads=reads, writes=writes, kind="d", nobar=nobar)

    def act(out, in_, func, reads, writes, scale=None, bias=None, accum=None, extra=()):
        def fn(eng):
            kw = {}
            if scale is not None:
                kw["scale"] = scale
            if bias is not None:
                kw["bias"] = bias
            if accum is not None:
                kw["accum_out"] = accum
            return eng.activation(out=out, in_=in_, func=func, **kw)
        return S.add("act", fn, reads=reads, writes=writes, extra=extra)

    def dve(fn, reads, writes, extra=()):
        return S.add("dve", fn, reads=reads, writes=writes, extra=extra)

    def tt(out, a, b, op, reads, writes):
        return dve(lambda e: e.tensor_tensor(out=out, in0=a, in1=b, op=op), reads, writes)

    def ts(out, a, s1, s2, op0, op1, reads, writes):
        if op1 is None:
            return dve(lambda e: e.tensor_scalar(out=out, in0=a, scalar1=s1, scalar2=None, op0=op0), reads, writes)
        return dve(lambda e: e.tensor_scalar(out=out, in0=a, scalar1=s1, scalar2=s2, op0=op0, op1=op1), reads, writes)

    def rsqrt(out, a, scale, bias, reads, writes):
        act(out, a, AF.Sqrt, reads, writes, scale=float(scale), bias=float(bias))
        dve(lambda e: e.reciprocal(out=out, in_=out), writes, writes)

    def stt(out, a, s, b, op0, op1, reads, writes):
        return dve(lambda e: e.scalar_tensor_tensor(out=out, in0=a, scalar=s, in1=b, op0=op0, op1=op1), reads, writes)

    def mm_group(out, pairs, reads, writes, first=True, last=True):
        def fn(eng):
            ins = None
            n = len(pairs)
            for k, (l, r) in enumerate(pairs):
                ins = eng.matmul(out, l, r, start=(first and k == 0), stop=(last and k == n - 1))
            return ins
        return S.add("pe", fn, reads=reads, writes=writes)

    def transposes(ptk, srcs, reads):
        def fn(eng):
            ins = None
            for k, s_ in enumerate(srcs):
                ins = eng.transpose(ptb[ptk][:, k * 128:(k + 1) * 128], s_, ident)
            return ins
        return S.add("pe", fn, reads=list(reads) + [res("ident")], writes=[res("pt", ptk)])

    def bcast_row(t, off, n):
        return bass.AP(t.ap().tensor, off, [[0, 128], [1, n]])

    def load_consts():
        dma("sp", ident, ident_d[:, :], writes=[res("ident")])
        dma("sp", masks, masks_d.ap().rearrange("p (a b) -> p a b", a=2), writes=[res("masks")])
        dma("sp", amat, amat_d.ap().rearrange("p (a b) -> p a b", a=16), writes=[res("amat")])
        dma("sp", assm, ass_d.ap().rearrange("p (a b) -> p a b", a=4), writes=[res("ass")])
        dma("sp", dqk, dqk_d[:, :], writes=[res("dqk")])
        dma("sp", bm8, bm8_d.ap().rearrange("p (a b) -> p a b", a=4), writes=[res("bm8")])
        dma("sp", flag[:, 0:1], flag_d[:, :], writes=[res("flag")])
        dma("sp", gcol, gcol_d[:, :], writes=[res("gcol")])

    wstate = {"issued": 0, "used": 0, "plan": []}

    def plan_block(src2d, c0, rows=D):
        wstate["plan"].append((src2d, c0, rows))

    def issue_upto(n):
        while wstate["issued"] < min(n, len(wstate["plan"])):
            src2d, c0, rows = wstate["plan"][wstate["issued"]]
            k = wstate["issued"] % 3
            wstate["issued"] += 1
            kc = rows // 128
            src = src2d.rearrange("(kc p) c -> p kc c", p=128)[:, :, c0:c0 + 256]
            dma("pool", wbuf[k][:, 0:kc, :], src, writes=[res("wbuf", k)], nobar=True)

    def load_block(src2d, c0, rows=D):
        n = wstate["used"]
        assert wstate["plan"][n][1] == c0 and wstate["plan"][n][0] is src2d, (n, c0)
        wstate["used"] += 1
        issue_upto(n + 3)
        return n % 3

    def phase_a_pre(l, i):
        x_i = xbuf[:, i, :]
        for s4 in range(4):
            act(junk, x_i[:, s4 * 512:(s4 + 1) * 512], AF.Square, reads=[res("x", i)], writes=[res("junk"), res("ssA", i)],
                accum=ssA[:, i * 4 + s4:i * 4 + s4 + 1])
        dve(lambda e: e.reduce_sum(out=ssS[:, i:i + 1], in_=ssA[:, i * 4:i * 4 + 4], axis=AX.X), [res("ssA", i)], [res("ssS", i)])
        rsqrt(rsA[:, i:i + 1], ssS[:, i:i + 1], 1.0 / D, EPS, [res("ssS", i)], [res("rsA", i)])
        b = HB.nxt()
        ts(hb[:, b, :], x_i, rsA[:, i:i + 1], None, ALU.mult, None, [res("x", i), res("rsA", i)], [res("hb", b)])
        return b

    def phase_a_pe(l, i, b, extra=()):
        for half in range(2):
            q = PT.nxt()
            transposes(q, [hb[:, b, (half * 8 + k) * 128:(half * 8 + k + 1) * 128] for k in range(8)], [res("hb", b)])
            gc = mk(gcol, l * 16 + half * 8, [[1, 8], [0, 128]])
            S.add("dve", (lambda e, half=half, q=q, gc=gc: e.tensor_tensor(
                out=hT[:, half * 8:half * 8 + 8, i * 128:(i + 1) * 128], in0=ptb[q][:, :].rearrange("p (a b) -> p a b", a=8), in1=gc, op=ALU.mult)),
                reads=[res("pt", q), res("gcol")], writes=[res("hT", i)], extra=extra)

    def phase_a(l, i):
        b = phase_a_pre(l, i)
        phase_a_pe(l, i, b)

    def proj(actT, actres, i, slot, nk=16):
        u = PJ.nxt()
        pairs = [(actT[:, kc, i * 128:(i + 1) * 128], wbuf[slot][:, kc, :]) for kc in range(nk)]
        mm_group(psh(u), pairs, [res(actres, i), res("wbuf", slot)], [BK(u)])
        return u

    def rope(u, i, l, h, is_k, dst):
        kind = 1 if i == 8 else 0
        col = (8 if is_k else 0) + h * 2 + kind
        sc = dqk[:, col:col + 1]
        b = RB.nxt()
        src = psh(u).rearrange("p (a b) -> p a b", a=2)
        cosb = mk(csr, (i % 2) * 256, [[0, 2], [1, 128]])
        sinb = mk(csr, (i % 2) * 256 + 128, [[0, 2], [1, 128]])
        t13 = T13[:, b, :].rearrange("p (a b) -> p a b", a=2)
        t42 = T42[:, b, :].rearrange("p (a b) -> p a b", a=2)
        stt(t13, src, sc, cosb, ALU.mult, ALU.mult, [BK(u), res("cs", i % 2), res("dqk")], [res("T13", b)])
        stt(t42, src, sc, sinb, ALU.mult, ALU.mult, [BK(u), res("cs", i % 2), res("dqk")], [res("T42", b)])
        return b

    def rope_fin(b, dst, dres):
        eng_ = "pool" if DBG.get("pool_rope", False) else "dve"
        S.add(eng_, lambda e: e.tensor_tensor(out=dst[:, 0:128], in0=T13[:, b, 0:128], in1=T42[:, b, 128:256], op=ALU.subtract),
              reads=[res("T13", b), res("T42", b)], writes=dres)
        S.add(eng_, lambda e: e.tensor_tensor(out=dst[:, 128:256], in0=T13[:, b, 128:256], in1=T42[:, b, 0:128], op=ALU.add),
              reads=[res("T13", b), res("T42", b)], writes=dres)

    def load_cs(i):
        dma("sp", csr[:, i % 2, :], cs_d[i, :, :], writes=[res("cs", i % 2)])

    def proj_split(actT, actres, i, slot):
        u = PJ.nxt()
        pairs = [(actT[:, kc, i * 128:(i + 1) * 128], wbuf[slot][:, kc, :]) for kc in range(16)]

        def pa():
            mm_group(psh(u), pairs[:8], [res(actres, i), res("wbuf", slot)], [BK(u)], first=True, last=False)

        def pb_():
            mm_group(psh(u), pairs[8:], [res(actres, i), res("wbuf", slot)], [BK(u)], first=False, last=True)
        return u, pa, pb_

    def k_pass(l, h, scan_h=None):
        slot = load_block(w_in_l[l], 3072 + 256 * h)
        prev = None
        for i in range(NT + 1):
            if i < NT:
                load_cs(i)
                u, pa, pb_ = proj_split(hT, "hT", i, slot)
                if scan_h is not None:
                    if i >= 1:
                        scan_part3a(l, scan_h, i - 1)
                    scan_part1(l, scan_h, i)
                pa()
                if scan_h is not None:
                    if i >= 1:
                        scan_part3b(l, scan_h, i - 1)
                    scan_part2(l, scan_h, i)
                pb_()
                b = rope(u, i, l, h, True, None)
                rope_fin(b, ktok[:, i, :], [res("ktok", i)])
            if prev is not None:
                j = prev
                q = PT.nxt()
                transposes(q, [ktok[:, j, 0:128], ktok[:, j, 128:256]], [res("ktok", j)])
                act(kT[:, :, j * 128:(j + 1) * 128], ptb[q][:, 0:256].rearrange("p (a b) -> p a b", a=2), AF.Copy,
                    [res("pt", q)], [res("kT", j)])
            prev = i if i < NT else None

    def v_pass(l, h, sample_h=None):
        slot = load_block(w_in_l[l], 4096 + 256 * h)
        spread = sample_h is not None
        for i in range(NT):
            u = proj(hT, "hT", i, slot)
            if spread and i == NT - 1:
                sample_end(l, sample_h)
            act(vall[:, i, :], psh(u), AF.Copy, [BK(u)], [res("vall", i)])
            if spread and i < 8:
                sample_batch(l, sample_h, 2 * i)
                sample_batch(l, sample_h, 2 * i + 1)
        if spread:
            scan_part3(l, sample_h, NT - 1)

    def u_exchange(l, h):
        g = GAM[h]
        UH = (10, 6)
        pairs0, pairs1 = [], []
        for c in range(8):
            b = c % 4
            kbuf = kUr[:, b, :] if b < 2 else kdr[:, b - 2, :]
            kres = res("kUr", b) if b < 2 else res("kdr", b - 2)
            act(kbuf, ktok[:, c, :], AF.Copy, [res("ktok", c)], [kres], scale=float(g ** (128 * (8 - c))))
            for kc2 in range(2):
                def fn(eng, c=c, kc2=kc2, kbuf=kbuf):
                    return eng.matmul(psh(UH[kc2]), kbuf[:, kc2 * 128:(kc2 + 1) * 128], vall[:, c, :], start=(c == 0), stop=(c == 7))
                S.add("pe", fn, reads=[kres, res("vall", c)], writes=[BK(UH[kc2])])
        for kc2 in range(2):
            act(Uev[:, kc2, :], psh(UH[kc2]), AF.Copy, [BK(UH[kc2])], [res("Uev", kc2)])
        dma("sp", bS_in[l][h].ap().rearrange("(kc p) e -> p kc e", p=128), Uev[:, :, :], reads=[res("Uev", 0), res("Uev", 1)], writes=[res("bSi", l, h)])

        def cc(eng):
            return eng.collective_compute("AllGather", ALU.bypass, replica_groups=PAIRS,
                                          ins=[bS_in[l][h].ap().opt()], outs=[bS_out[l][h].ap().opt()])
        S.add("pool", cc, reads=[res("bSi", l, h)], writes=[res("bSo", l, h)], kind="cc")

    def s_init(l, h):
        dma("sp", Sf[:, :, :], bS_out[l][h].ap()[0:256, :].rearrange("(kc p) e -> p kc e", p=128), reads=[res("bSo", l, h)], writes=[res("Sf")])
        ts(Sf[:, :, :], Sf[:, :, :], flag[:, 0:1], None, ALU.mult, None, [res("Sf"), res("flag")], [res("Sf")])
        act(Sbf[:, 0, :, :], Sf[:, :, :], AF.Copy, [res("Sf")], [res("Sbf", 0)])

    def gr_pass(l, h, sample_h=None):
        slot = load_block(w_in_l[l], 5120 + 256 * h)
        spread = sample_h is not None
        for i in range(NT):
            u = proj(hT, "hT", i, slot)
            if spread and i == NT - 1:
                sample_end(l, sample_h)
                scan_part3(l, sample_h, NT - 1)
            act(sgr[:, i, :], psh(u), AF.Silu, [BK(u)], [res("sgr", i)])
            if spread and i < 7:
                sample_batch(l, sample_h, 9 + i)

    QTU = {"ids": []}

    def qT_(c, kc):
        return qTall[:, kc, c * 128:(c + 1) * 128]

    def scan_part1(l, h, c):
        g = GAM[h]
        QTU["ids"].append(mm_group(psh(6)[:, 0:128], [(kT[:, kc, c * 128:(c + 1) * 128], qT_(c, kc)) for kc in range(2)],
                                   [res("kT", c), res("qTall", c)], [BK(6)]))
        sm = c % 2
        tt(scm[:, sm, :], psh(6)[:, 0:128], masks[:, 1 if c == 8 else 0, :], ALU.mult, [BK(6), res("masks")], [res("scm", sm)])
        if c < 8:
            kb = c % 2
            act(kdr[:, kb, :], ktok[:, c, :], AF.Copy, [res("ktok", c)], [res("kdr", kb)], scale=float(g ** 128))
        else:
            qm_dst = mk(qm, 0, [[16 * 128, 2], [136, 16], [1, 8]])
            q_src = mk(qTall, 8 * 128, [[TOK, 2], [8, 16], [1, 8]])
            QTU["ids"].append(dve(lambda e: e.tensor_copy(out=qm_dst, in_=q_src), [res("qTall", 8)], [res("qm")]))
            k_b = mk(ktok, 8 * 256, [[0, 16], [1, 256]])
            m_b = mk(bm8, h * 16, [[1, 16], [0, 256]])
            tt(km[:, :, :], k_b, m_b, ALU.mult, [res("ktok", 8), res("bm8")], [res("km")])

    def scan_part2(l, h, c):
        g = GAM[h]
        sm = c % 2
        if c < 8:
            sb_cur, sb_nxt = c % 2, (c + 1) % 2
            kb = c % 2
            pairs = [(scm[:, sm, :], vall[:, c, :])] + [(qT_(c, kc), Sbf[:, sb_cur, kc, :]) for kc in range(2)]
            QTU["ids"].append(mm_group(psh(8), pairs, [res("scm", sm), res("vall", c), res("qTall", c), res("Sbf", sb_cur)], [BK(8)]))
            for kc2 in range(2):
                mm_group(psh(10 + kc2), [(kdr[:, kb, kc2 * 128:(kc2 + 1) * 128], vall[:, c, :])], [res("kdr", kb), res("vall", c)], [BK(10 + kc2)])
            stt(Sf[:, :, :], Sf[:, :, :], float(g ** 128), psb[5][:, :].rearrange("p (a b) -> p a b", a=2), ALU.mult, ALU.add,
                [res("Sf"), BK(10), BK(11)], [res("Sf")])
            act(Sbf[:, sb_nxt, :, :], Sf[:, :, :], AF.Copy, [res("Sf")], [res("Sbf", sb_nxt)])
            if c == 7:
                dma("sp", nrp[l, h].rearrange("(kc p) e -> p kc e", p=128), Sf[:, :, :], reads=[res("Sf")])
        else:
            sample_begin(l, h)
            if SAMPLE_INLINE["on"]:
                for b in range(16):
                    sample_batch(l, h, b)
                sample_end(l, h)
            return
        y_finish(c)

    SAMPLE_INLINE = {"on": False}

    def y_finish(c):
        yb = c % 2
        act(junk[:, 0:256], psh(8), AF.Square, [BK(8)], [res("junk"), res("ssy", yb)], accum=ssy[:, yb:yb + 1])
        act(Uev[:, yb, :], psh(8), AF.Copy, [BK(8)], [res("Uev", yb)])
        act(rsy[:, yb:yb + 1], ssy[:, yb:yb + 1], AF.Sqrt, [res("ssy", yb)], [res("rsy", yb)], scale=1.0 / 256.0, bias=float(EPS))

    def sample_begin(l, h):
        c = 8
        sm = c % 2
        sview = sret[l].rearrange("b h (kc p) e -> b h p kc e", p=128)
        act(v8s[:, :], vall[:, c, :], AF.Copy, [res("vall", c)], [res("v8s")])
        S.add("pe", lambda eng: eng.matmul(psh(8), scm[:, sm, :], v8s[:, :], start=True, stop=False),
              reads=[res("scm", sm), res("v8s")], writes=[BK(8)])
        for b in range(3):
            dma("sp", Sin[:, b % 3, :, :], sview[b, h], writes=[res("Sin", b % 3)])

    def sample_batch(l, h, b):
        sb_ = b % 3
        act(Sbb[:, sb_, :, :], Sin[:, sb_, :, :], AF.Copy, [res("Sin", sb_)], [res("Sbb", sb_)])
        if b >= 1:
            sample_stage_b(l, h, b - 1)

    def sample_stage_b(l, h, b):
        g = GAM[h]
        NB = 16
        sview = sret[l].rearrange("b h (kc p) e -> b h p kc e", p=128)
        oview = nrs[l].rearrange("b h (kc p) e -> b h p kc e", p=128)
        sb_ = b % 3
        so_ = b % 2

        def yf(eng, b=b, sb_=sb_):
            ins = None
            for kc in range(2):
                ins = eng.matmul(psh(8), qm[:, kc, b, :], Sbb[:, sb_, kc, :], start=False, stop=(b == NB - 1 and kc == 1))
            return ins
        S.add("pe", yf, reads=[res("qm"), res("Sbb", sb_), BK(8)], writes=[BK(8)])
        bank = 5 if b % 2 == 0 else 3
        for kc2 in range(2):
            mm_group(psh(2 * bank + kc2), [(km[:, b, kc2 * 128:(kc2 + 1) * 128], v8s[:, :])], [res("km"), res("v8s")],
                     [BK(2 * bank + kc2)])
        stt(Sout[:, so_, :, :], Sin[:, sb_, :, :], float(g ** 8), psb[bank][:, :].rearrange("p (a b) -> p a b", a=2), ALU.mult, ALU.add,
            [res("Sin", sb_), BK(2 * bank), BK(2 * bank + 1)], [res("Sout", so_)])
        dma("sp", oview[b, h], Sout[:, so_, :, :], reads=[res("Sout", so_)])
        if b + 3 < NB:
            dma("sp", Sin[:, sb_, :, :], sview[b + 3, h], writes=[res("Sin", sb_)])

    def sample_end(l, h):
        sample_stage_b(l, h, 15)
        y_finish(8)

    def scan_part3(l, h, c):
        scan_part3a(l, h, c)
        scan_part3b(l, h, c)

    def scan_part3a(l, h, c):
        yb = c % 2
        dve(lambda e: e.reciprocal(out=rsy[:, yb:yb + 1], in_=rsy[:, yb:yb + 1]), [res("rsy", yb)], [res("rsy", yb)])
        stt(rety[:, yb, :], Uev[:, yb, :], rsy[:, yb:yb + 1], sgr[:, c, :], ALU.mult, ALU.mult,
            [res("Uev", yb), res("rsy", yb), res("sgr", c)], [res("rety", yb)])

    def scan_part3b(l, h, c):
        yb = c % 2
        q = PT.nxt()
        transposes(q, [rety[:, yb, 0:128], rety[:, yb, 128:256]], [res("rety", yb)])
        act(mixT[:, 8 + 2 * h:10 + 2 * h, c * 128:(c + 1) * 128], ptb[q][:, 0:256].rearrange("p (a b) -> p a b", a=2), AF.Copy,
            [res("pt", q)], [res("mixT", c)])

    def scan_alone(l, h):
        inline = not DBG.get("sample_in_c", True)
        SAMPLE_INLINE["on"] = inline
        for c in range(SCAN_DONE["n"], NT):
            if c >= 1:
                scan_part3a(l, h, c - 1)
            scan_part1(l, h, c)
            if c >= 1:
                scan_part3b(l, h, c - 1)
            scan_part2(l, h, c)
        SCAN_DONE["n"] = 0
        if inline:
            scan_part3(l, h, NT - 1)
        SAMPLE_INLINE["on"] = False

    SCAN_DONE = {"n": 0}

    def q_pass(l, h, uall_users):
        QTU["ids"] = []
        s_init(l, h)
        tail = (h == 3) and DBG.get("q_scan", True)
        SCAN_DONE["n"] = 0
        slot = load_block(w_in_l[l], 2048 + 256 * h)
        prev = None
        for i in range(NT + 1):
            c = i - 2
            if tail and 0 <= c < 8:
                if c >= 1:
                    scan_part3a(l, h, c - 1)
                scan_part1(l, h, c)
            if i < NT:
                load_cs(i)
                u = proj(hT, "hT", i, slot)
            if tail and 0 <= c < 8:
                if c >= 1:
                    scan_part3b(l, h, c - 1)
                scan_part2(l, h, c)
                SCAN_DONE["n"] = c + 1
            if i < NT:
                b = rope(u, i, l, h, False, None)
                qb = i % 2
                rope_fin(b, qtok[:, qb, :], [res("qtok", qb)])
            if prev is not None:
                j = prev
                qb2 = j % 2
                q = PT.nxt()
                transposes(q, [qtok[:, qb2, 0:128], qtok[:, qb2, 128:256]], [res("qtok", qb2)])
                act(qTall[:, :, j * 128:(j + 1) * 128], ptb[q][:, 0:256].rearrange("p (a b) -> p a b", a=2), AF.Copy, [res("pt", q)],
                    [res("qTall", j)], extra=uall_users)
            prev = i if i < NT else None

    def pool_prep(l):
        dma("pool", wpool[:, :, :], w_pool[l].rearrange("g (kc p) d -> p (g kc) d", p=128), writes=[res("wpool")])
        dma("sp", pst, bcast_row(pool_scale, l * 1024, 1024), writes=[res("pst")])
        w4 = mk(wpool, 0, [[512, 4], [256, 2], [1, 256]])
        p4 = mk(pst, 0, [[256, 4], [0, 2], [1, 256]])
        tt(w4, w4, p4, ALU.mult, [res("wpool"), res("pst")], [res("wpool")])
        dma("sp", nps[l][:, 0:7, :], spool[l][:, 8:15, :])

    UALLU = {"ids": []}

    def u_pass(l, g):
        UALLU["ids"] = []
        slot = load_block(w_in_l[l], 256 * g)
        for hf in range(2):
            src = spool[l][8 * hf:8 * hf + 8].rearrange("b r c -> (b r) c")[:, 256 * g:256 * g + 256]
            dma("pool", mk(bufS, hf * 256, [[1, 256]], parts=120), src, writes=[res("bufS", hf)])
        for i in [7, 0, 1, 2, 3, 4, 5, 6, 8]:
            u = proj(hT, "hT", i, slot)
            act(uall[:, i, :], psh(u), AF.Copy, [BK(u)], [res("uall", i)], extra=list(QTU["ids"]))
            if i >= 7:
                act(uf32[:, i - 7, :], psh(u), AF.Copy, [BK(u)], [res("uf32", i - 7)])
            if i == 7:
                dma("sp", npp[l][:, 256 * g:256 * g + 256], uf32[113:128, 0, :], reads=[res("uf32", 0)])
                UALLU["ids"].append(dma("sp", bU_in[l][g].ap(), uall[:, 7, :], reads=[res("uall", 7)], writes=[res("bUi", l, g)]))

                def cc(eng):
                    return eng.collective_compute("AllGather", ALU.bypass, replica_groups=PAIRS,
                                                  ins=[bU_in[l][g].ap().opt()], outs=[bU_out[l][g].ap().opt()])
                S.add("pool", cc, reads=[res("bUi", l, g)], writes=[res("bUo", l, g)], kind="cc")
            if i == 8:
                for b in range(16):
                    dma("sp", nps[l][b, 7:15, 256 * g:256 * g + 256], uf32[8 * b:8 * b + 8, 1, :], reads=[res("uf32", 1)])

    def gp_pass_pool(l, g):
        slot = load_block(w_in_l[l], 1024 + 256 * g)
        order = [1, 2, 3, 4, 5, 6, 7, 8, 0]
        prev = None
        for n in range(NT + 1):
            if n < NT:
                i = order[n]
                u = proj(hT, "hT", i, slot)
                sb_ = n % 2
                act(sgpr[:, sb_, :], psh(u), AF.Silu, [BK(u)], [res("sgpr", sb_)])
            if prev is not None:
                if order[prev] == 0:
                    dma("sp", uprev_raw, bU_out[l][g].ap()[0:128, :], reads=[res("bUo", l, g)], writes=[res("uprev_raw")])
                    ts(uprevX, uprev_raw, flag[:, 0:1], None, ALU.mult, None, [res("uprev_raw"), res("flag")], [res("uprevX")])
                pool_tile(l, g, order[prev], prev % 2)
            prev = n if n < NT else None
        pool_tile_t(l, g, *PTL["prev"])
        PTL["prev"] = None

    def pool_tile(l, g, i, sgb):
        pbuf = sgb
        if i == 0:
            kind = 2
        elif i == 8:
            kind = 3
        else:
            kind = 0
        for half in range(2):
            out = psh(6)[:, half * 128:(half + 1) * 128]
            pairs = [(uall[:, i, half * 128:(half + 1) * 128], amat[:, g * 4 + kind, :])]
            rd = [res("uall", i), res("amat")]
            if i == 0:
                pairs.append((uprevX[:, half * 128:(half + 1) * 128], amat[:, g * 4 + 1, :]))
                rd.append(res("uprevX"))
            elif i < 8:
                pairs.append((uall[:, i - 1, half * 128:(half + 1) * 128], amat[:, g * 4 + 1, :]))
                rd.append(res("uall", i - 1))
            if i < 8:
                UALLU["ids"].append(mm_group(out, pairs, rd, [BK(6)]))
            else:
                def fn(eng, half=half, out=out, pairs=pairs):
                    ins = None
                    for hf in range(2):
                        eng.matmul(out[:, hf * 64:(hf + 1) * 64], pairs[0][0], pairs[0][1][:, hf * 64:(hf + 1) * 64], start=True, stop=False)
                        ins = eng.matmul(out[:, hf * 64:(hf + 1) * 64], mk(bufS, hf * 256 + half * 128, [[1, 128]], parts=120),
                                         mk(assm, g * 64, [[1, 64]], parts=120), start=False, stop=True)
                    return ins
                UALLU["ids"].append(S.add("pe", fn, reads=rd + [res("bufS", 0), res("bufS", 1), res("ass")], writes=[BK(6)]))
        act(pooledT[:, pbuf, :, :], psh(6).rearrange("p (a b) -> p a b", a=2), AF.Copy, [BK(6)], [res("pooledT", pbuf)])
        if PTL["prev"] is not None:
            pool_tile_t(l, g, *PTL["prev"])
        mm_group(psh(8), [(pooledT[:, pbuf, half, :], wpool[:, g * 2 + half, :]) for half in range(2)],
                 [res("pooledT", pbuf), res("wpool")], [BK(8)])
        tt(pooly[:, pbuf, :], psh(8), sgpr[:, sgb, :], ALU.mult, [BK(8), res("sgpr", sgb)], [res("pooly", pbuf)])
        PTL["prev"] = (i, pbuf)

    PTL = {"prev": None}

    def pool_tile_t(l, g, i, pbuf):
        q = PT.nxt()
        transposes(q, [pooly[:, pbuf, 0:128], pooly[:, pbuf, 128:256]], [res("pooly", pbuf)])
        act(mixT[:, 2 * g:2 * g + 2, i * 128:(i + 1) * 128], ptb[q][:, 0:256].rearrange("p (a b) -> p a b", a=2), AF.Copy,
            [res("pt", q)], [res("mixT", i)])

    def phase_b(l):
        dve(lambda e: e.memset(qm[:, :, :, :], 0.0), [], [res("qm")])
        pool_prep(l)
        QTU["ids"] = []
        cut = DBG.get("cut", 10 ** 9)
        n = [0]

        def tick():
            n[0] += 1
            if n[0] >= cut:
                raise StopIteration
        for s_ in range(4):
            k_pass(l, s_, scan_h=(s_ - 1 if s_ >= 1 else None)); tick()
            v_pass(l, s_, sample_h=(s_ - 1 if s_ >= 1 else None)); tick()
            u_exchange(l, s_); tick()
            gr_pass(l, s_); tick()
            u_pass(l, s_); tick()
            gp_pass_pool(l, s_); tick()
            q_pass(l, s_, list(UALLU["ids"])); tick()
        scan_alone(l, 3)

    ZREAD = {"ids": []}

    def phase_c(l):
        xsrc = xin if l == 0 else xpark
        ride = DBG.get("sample_in_c", True)

        FX = {"ids": (), "n": 0}

        def load_x1(i, extra=()):
            S.add("sp", (lambda eng, i=i: eng.dma_start(out=xbuf[:, i, :], in_=xsrc[i * 128:(i + 1) * 128, :])),
                  writes=[res("x", i)], kind="d", extra=extra)

        def load_x(extra=()):
            for i in range(NT):
                load_x1(i, extra)
        if not ride:
            load_x()
        dma("pool", pbf[:, :, :], pin[l].rearrange("(i p) c -> p i c", p=128), writes=[res("pbf")])
        dma("sp", gtab[:, 0, :], bcast_row(post_g, l * D, 256), writes=[res("gtab", 0)])
        for j in range(8):
            slot = load_block(w_out_l[l], 256 * j)
            gb = j % 2
            if j + 1 < 8:
                dma("sp", gtab[:, (j + 1) % 2, :], bcast_row(post_g, l * D + 256 * (j + 1), 256), writes=[res("gtab", (j + 1) % 2)])
            for i in range(NT):
                if ride and j == 0 and i == NT - 1:
                    sample_end(l, 3)
                    scan_part3(l, 3, NT - 1)
                u = proj(mixT, "mixT", i, slot)
                sq = act(junk[:, 0:256], psh(u), AF.Square, [BK(u)], [res("junk"), res("ssz", i)], accum=ssz[:, i * 8 + j:i * 8 + j + 1])
                S.add("dve", (lambda e, i=i, j=j, u=u, gb=gb: e.tensor_tensor(out=zb[:, i, j * 256:(j + 1) * 256], in0=psh(u), in1=gtab[:, gb, :], op=ALU.mult)),
                      reads=[BK(u), res("gtab", gb)], writes=[res("z", i)], extra=[sq])
                if ride and j == 0 and i < 8:
                    sample_batch(l, 3, 2 * i)
                    sample_batch(l, 3, 2 * i + 1)
                if ride and 1 <= j <= 3 and i in (0, 3, 6):
                    load_x1(FX["n"], FX["ids"])
                    FX["n"] += 1
            if ride and j == 0:
                FX["ids"] = S.fence()
        if DBG.get("cd_merge", True):
            phase_d_pre(l)
        ZREAD["ids"] = []
        for i in range(NT):
            dve(lambda e, i=i: e.reduce_sum(out=rsz[:, i:i + 1], in_=ssz[:, i * 8:i * 8 + 8], axis=AX.X), [res("ssz", i)], [res("rsz", i)])
            rsqrt(rsz[:, i:i + 1], rsz[:, i:i + 1], 1.0 / D, EPS, [res("rsz", i)], [res("rsz", i)])
        for i in range(NT):
            ZREAD["ids"].append(stt(xbuf[:, i, :], zb[:, i, :], rsz[:, i:i + 1], xbuf[:, i, :], ALU.mult, ALU.add,
                                    [res("z", i), res("rsz", i), res("x", i)], [res("x", i)]))
            b = HB.nxt()
            act(hb[:, b, :], xbuf[:, i, :], AF.Copy, [res("x", i)], [res("hb", b)])
            for half in range(2):
                q = PT.nxt()
                transposes(q, [hb[:, b, (half * 8 + k) * 128:(half * 8 + k + 1) * 128] for k in range(8)], [res("hb", b)])
                act(mixT[:, half * 8:half * 8 + 8, i * 128:(i + 1) * 128], ptb[q][:, :].rearrange("p (a b) -> p a b", a=8), AF.Copy,
                    [res("pt", q)], [res("mixT", i)])

    PA = {"b": 0}

    def load_wple(l, idx):
        j = idx % 8
        wb = idx % 2
        src = w_ple[l].rearrange("(kc p) c -> p kc c", p=128)[:, :, 256 * j:256 * j + 256]
        dma("pool", wple[:, wb, :, :], src, writes=[res("wple", wb)])

    def phase_d_pre(l):
        for i in range(NT):
            q = PT.nxt()
            transposes(q, [pbf[:, i, 0:128], pbf[:, i, 128:256]], [res("pbf")])
            act(pT[:, :, i * 128:(i + 1) * 128], ptb[q][:, 0:256].rearrange("p (a b) -> p a b", a=2), AF.Copy, [res("pt", q)], [res("pT", i)])

        load_wple(l, 0)
        for j in range(8):
            load_wple(l, j + 1)
            wb = j % 2
            for i in range(NT):
                u = PJ.nxt()
                mm_group(psh(u), [(pT[:, kc, i * 128:(i + 1) * 128], wple[:, wb, kc, :]) for kc in range(2)],
                         [res("pT", i), res("wple", wb)], [BK(u)])
                act(junk[:, 0:256], psh(u), AF.Square, [BK(u)], [res("junk"), res("ssp", i)], accum=ssp[:, i * 8 + j:i * 8 + j + 1])
        for i in range(NT):
            dve(lambda e, i=i: e.reduce_sum(out=rsp[:, i:i + 1], in_=ssp[:, i * 8:i * 8 + 8], axis=AX.X), [res("ssp", i)], [res("rsp", i)])
            rsqrt(rsp[:, i:i + 1], rsp[:, i:i + 1], 1.0 / D, EPS, [res("rsp", i)], [res("rsp", i)])

    def phase_d(l):
        if not DBG.get("cd_merge", True):
            phase_d_pre(l)
        for j in range(8):
            if j < 7:
                load_wple(l, 8 + j + 1)
            slot = load_block(w_gate_l[l], 256 * j)
            wb = j % 2
            gb = (2 * j) % 4
            dma("sp", gtab[:, gb, :], bcast_row(b_gate, l * D + 256 * j, 256), writes=[res("gtab", gb)])
            dma("sp", gtab[:, gb + 1, :], bcast_row(ple_g, l * D + 256 * j, 256), writes=[res("gtab", gb + 1)])
            for i in range(NT):
                ua = proj(mixT, "mixT", i, slot)
                ub = PJ.nxt()
                mm_group(psh(ub), [(pT[:, kc, i * 128:(i + 1) * 128], wple[:, wb, kc, :]) for kc in range(2)],
                         [res("pT", i), res("wple", wb)], [BK(ub)])
                t1 = TMP.nxt()
                t2 = TMP.nxt()
                tt(tmpf[:, t1, :], psh(ua), gtab[:, gb, :], ALU.add, [BK(ua), res("gtab", gb)], [res("tmpf", t1)])
                act(tmpf[:, t1, :], tmpf[:, t1, :], AF.Sigmoid, [res("tmpf", t1)], [res("tmpf", t1)])
                stt(tmpf[:, t2, :], psh(ub), rsp[:, i:i + 1], gtab[:, gb + 1, :], ALU.mult, ALU.mult,
                    [BK(ub), res("rsp", i), res("gtab", gb + 1)], [res("tmpf", t2)])
                tt(tmpf[:, t1, :], tmpf[:, t1, :], tmpf[:, t2, :], ALU.mult, [res("tmpf", t1), res("tmpf", t2)], [res("tmpf", t1)])
                tt(xbuf[:, i, j * 256:(j + 1) * 256], xbuf[:, i, j * 256:(j + 1) * 256], tmpf[:, t1, :], ALU.add,
                   [res("x", i), res("tmpf", t1)], [res("x", i)])
                if j == 7:
                    if l == 0:
                        if i >= 1:
                            phase_a_pe(1, i - 1, PA["b"], extra=list(ZREAD["ids"]))
                        PA["b"] = phase_a_pre(1, i)
                        dma("sp", xpark[i * 128:(i + 1) * 128, :], xbuf[:, i, :], reads=[res("x", i)], writes=[res("xpark", i)])
                        if i == NT - 1:
                            phase_a_pe(1, i, PA["b"], extra=list(ZREAD["ids"]))
                    else:
                        dma("sp", yout[i * 128:(i + 1) * 128, :], xbuf[:, i, :], reads=[res("x", i)])

    w_in_l = [w_in[l] for l in range(DEPTH)]
    w_out_l = [w_out[l] for l in range(DEPTH)]
    w_gate_l = [w_gate[l] for l in range(DEPTH)]
    for l in range(DEPTH):
        for s_ in range(4):
            for c0 in (3072 + 256 * s_, 4096 + 256 * s_, 5120 + 256 * s_, 256 * s_, 1024 + 256 * s_, 2048 + 256 * s_):
                plan_block(w_in_l[l], c0)
        for j in range(8):
            plan_block(w_out_l[l], 256 * j)
        for j in range(8):
            plan_block(w_gate_l[l], 256 * j)
    issue_upto(3)
    load_consts()
    for i in range(NT):
        dma("sp", xbuf[:, i, :], xin[i * 128:(i + 1) * 128, :], writes=[res("x", i)])
    for i in range(NT):
        if stop_after != "consts":
            phase_a(0, i)
    stages = []
    for l in range(DEPTH):
        stages += [("B", l), ("C", l), ("D", l)]
    if stop_after in ("consts", "A"):
        stages = None
    for (ph, l) in (stages or []):
        if not (ph == "C" and DBG.get("sample_in_c", True)) and not (ph == "D" and DBG.get("cd_merge", True)):
            S.barrier()
        if ph == "B":
            try:
                phase_b(l)
            except StopIteration:
                break
        elif ph == "C":
            phase_c(l)
        else:
            phase_d(l)
        if stop_after == (ph, l):
            break
    S.finish()

    with nc.Block() as block:
        S.emit(nc, block, sems, dsems, ccsem)
    es.close()
    return nc


def _bf(a):
    return np.ascontiguousarray(a.astype(ml_dtypes.bfloat16))


def _tables(core):
    odd = core % 2
    start = 1024 * odd
    half = 128
    inv = (np.float32(10000.0) ** (-(np.arange(half, dtype=np.float32)) / np.float32(half))).astype(np.float32)
    cs = np.zeros((NT, 128, 256), np.float32)
    p = np.arange(128)
    for i in range(NT):
        pos = (start + 128 * i + p) if i < 8 else (16384 + (p % 8))
        ang = (pos.astype(np.float32)[:, None] * inv[None, :]).astype(np.float32)
        cs[i, :, :128] = np.cos(ang.astype(np.float64))
        cs[i, :, 128:] = np.sin(ang.astype(np.float64))
    dqk = np.zeros((128, 16), np.float64)
    bm8 = np.zeros((128, 4, 16), np.float64)
    for h in range(4):
        g = GAM[h]
        dqk[:, h * 2 + 0] = g ** (p + 1.0)
        dqk[:, h * 2 + 1] = g ** ((p % 8) + 1.0)
        dqk[:, 8 + h * 2 + 0] = g ** (-(p + 1.0)) / 16.0
        dqk[:, 8 + h * 2 + 1] = g ** (-((p % 8) + 1.0)) / 16.0
        for b in range(16):
            bm8[:, h, b] = (p // 8 == b) * (g ** 8)
    masks = np.zeros((128, 2, 128), np.float32)
    jj, ii = np.meshgrid(p, p, indexing="ij")
    masks[:, 0, :] = (ii >= jj)
    masks[:, 1, :] = (ii >= jj) & (ii // 8 == jj // 8)
    amat = np.zeros((128, 4, 4, 128), np.float64)
    ass = np.zeros((128, 4, 64), np.float64)
    ss, tt_ = np.meshgrid(p, p, indexing="ij")
    for g, w in enumerate(WINS):
        inwin = (ss <= tt_) & (ss >= tt_ - w + 1)
        amat[:, g, 0, :] = inwin / w - (ss == tt_)
        amat[:, g, 1, :] = ((ss - 128) >= (tt_ - w + 1)) / w
        if odd:
            amat[:, g, 2, :] = amat[:, g, 0, :]
        else:
            cnt = np.minimum(tt_ + 1, w)
            amat[:, g, 2, :] = inwin / cnt - (ss == tt_)
        sb_, st_ = ss // 8, ss % 8
        tb_, ttt = tt_ // 8, tt_ % 8
        inw = (st_ <= ttt) & (st_ >= ttt - w + 1) & (sb_ == tb_)
        amat[:, g, 3, :] = inw / w - (ss == tt_)
        for r in range(120):
            b, rr = r // 15, r % 15
            for t in range(8):
                if rr >= 16 + t - w:
                    ass[r, g, b * 8 + t] = 1.0 / w
    return dict(
        cs=cs, dqk=dqk.astype(np.float32), bm8=bm8.reshape(128, 64).astype(np.float32),
        masks=_bf(masks.reshape(128, 256)), amat=_bf(amat.reshape(128, 2048)), ass=_bf(ass.reshape(128, 256)),
        ident=_bf(np.eye(128, dtype=np.float32)), flag=np.full((128, 1), float(odd), np.float32),
    )


_NC_CACHE = {}


def kernel(x_prompt, x_sample, state_ret, state_pool, p_prompt, p_sample,
           w_in, w_pool, pool_scale, w_out, pre_g, post_g, w_ple, ple_g, w_ple_gate, b_ple_gate, _stop_after=None):
    f = lambda a: np.ascontiguousarray(np.asarray(a, dtype=np.float32))
    x_prompt, x_sample, state_ret, state_pool, p_prompt, p_sample = map(f, (x_prompt, x_sample, state_ret, state_pool, p_prompt, p_sample))
    w_in, w_pool, pool_scale, w_out, pre_g, post_g, w_ple, ple_g, w_ple_gate, b_ple_gate = map(
        f, (w_in, w_pool, pool_scale, w_out, pre_g, post_g, w_ple, ple_g, w_ple_gate, b_ple_gate))
    key = _stop_after
    if key not in _NC_CACHE:
        _NC_CACHE[key] = build_program(_stop_after)
    nc = _NC_CACHE[key]
    gcol = np.ascontiguousarray(pre_g.reshape(DEPTH, 16, 128).transpose(2, 0, 1).reshape(128, DEPTH * 16))
    in_maps = []
    for c in range(NCORE):
        seq, hf = c // 2, c % 2
        xin = np.concatenate([x_prompt[seq, hf * 1024:(hf + 1) * 1024], x_sample[16 * c:16 * c + 16].reshape(128, D)], axis=0)
        pin = np.concatenate([p_prompt[:, seq, hf * 1024:(hf + 1) * 1024], p_sample[:, 16 * c:16 * c + 16].reshape(DEPTH, 128, 256)], axis=1)
        m = dict(xin=np.ascontiguousarray(xin), pin=np.ascontiguousarray(pin),
                 sret=np.ascontiguousarray(state_ret[:, 16 * c:16 * c + 16]), spool=np.ascontiguousarray(state_pool[:, 16 * c:16 * c + 16]),
                 w_in=w_in, w_pool=w_pool, pool_scale=pool_scale, w_out=w_out, gcol=gcol, post_g=post_g, w_ple=w_ple, ple_g=ple_g,
                 w_gate=w_ple_gate, b_gate=b_ple_gate)
        m.update(_tables(c))
        in_maps.append(m)
    if DBG.get("return_maps"):
        return nc, in_maps
    res_ = run_bass_kernel_spmd(nc, in_maps, core_ids=list(range(NCORE)))
    R = res_.results
    y_prompt = np.zeros((4, 2048, D), np.float32)
    y_sample = np.zeros((128, 8, D), np.float32)
    nrp = np.zeros((DEPTH, 4, 4, 256, 256), np.float32)
    npp = np.zeros((DEPTH, 4, 15, 1024), np.float32)
    nrs = np.zeros((DEPTH, 128, 4, 256, 256), np.float32)
    nps = np.zeros((DEPTH, 128, 15, 1024), np.float32)
    for c in range(NCORE):
        seq, hf = c // 2, c % 2
        yo = np.asarray(R[c]["yout"])
        y_prompt[seq, hf * 1024:(hf + 1) * 1024] = yo[:1024]
        y_sample[16 * c:16 * c + 16] = yo[1024:].reshape(16, 8, D)
        if hf == 1:
            nrp[:, seq] = np.asarray(R[c]["nrp"])
            npp[:, seq] = np.asarray(R[c]["npp"])
        nrs[:, 16 * c:16 * c + 16] = np.asarray(R[c]["nrs"])
        nps[:, 16 * c:16 * c + 16] = np.asarray(R[c]["nps"])
    return (y_prompt, y_sample, nrp, npp, nrs, nps)
```
